# Optimizing a Trainium2 kernel written in Bass

```python
import math
import jax, jax.numpy as jnp
from jax import lax
import numpy as np

D_MODEL = 2048
BATCH = 2
SEQ = 4096
DEPTH = 1
DEC_BATCH = 4
DEC_SEQ = 4096
PAST_LEN = 128

ATT_HEADS = 8
ATT_QK_DIM = 64
ATT_V_DIM = 2 * ATT_QK_DIM
ATT_WIDTH = ATT_HEADS * ATT_V_DIM
MLSTM_HEADS = 4
MLSTM_HEAD_DIM = 256
MLSTM_WIDTH = MLSTM_HEADS * MLSTM_HEAD_DIM
MLSTM_CHUNK = 128
CONV_WIDTH = 5
MIX_WIDTH = ATT_WIDTH + MLSTM_WIDTH
ATT_QK_COLS = ATT_HEADS * 2 * ATT_QK_DIM
N_GATES = 4 * MLSTM_HEADS
SPLIT_SIZES = (ATT_QK_COLS, ATT_QK_COLS, ATT_WIDTH,
               MLSTM_WIDTH, MLSTM_WIDTH, MLSTM_WIDTH, MLSTM_WIDTH, N_GATES)
IN_WIDTH = sum(SPLIT_SIZES)
FFN_DIM = 5632
REL_BUCKETS = 32
REL_MAX_DIST = 128
Q_BLOCK = 128
EPS = 1e-6

kernel_name = "hymba_diffattn_mlstm_encoder"


def rmsnorm(x, g):
    xf = x.astype(jnp.float32)
    y = xf * lax.rsqrt(jnp.mean(xf * xf, axis=-1, keepdims=True) + EPS)
    return (y * g.astype(jnp.float32)).astype(x.dtype)


def swiglu(x, w_gate, w_up, w_down):
    return (jax.nn.silu(x @ w_gate) * (x @ w_up)) @ w_down


def rel_bucket(rel):
    nb = REL_BUCKETS // 2
    max_exact = nb // 2
    ret = jnp.where(rel > 0, nb, 0)
    n = jnp.abs(rel)
    nf = jnp.maximum(n, 1).astype(jnp.float32)
    large = max_exact + (jnp.log(nf / max_exact) / math.log(REL_MAX_DIST / max_exact)
                         * (nb - max_exact)).astype(jnp.int32)
    large = jnp.minimum(large, nb - 1)
    return ret + jnp.where(n < max_exact, n, large)


def diff_attention(q, k, v, rel_bias, lam, lambda_init, head_g):
    B, S = q.shape[0], q.shape[1]
    nb = S // Q_BLOCK
    qb = (q * (ATT_QK_DIM ** -0.5)).reshape(B, nb, Q_BLOCK, ATT_HEADS, 2, ATT_QK_DIM)
    qb = jnp.moveaxis(qb, 1, 0)
    key_pos = jnp.arange(S)

    def block(args):
        i, q_i = args
        q_pos = i * Q_BLOCK + jnp.arange(Q_BLOCK)
        bucket = rel_bucket(key_pos[None, :] - q_pos[:, None])
        bias = rel_bias[bucket].astype(jnp.float32)
        bias = bias.reshape(Q_BLOCK, S, ATT_HEADS, 2).transpose(2, 3, 0, 1)
        logits = jnp.einsum('bqhcd,bkhcd->bhcqk', q_i, k).astype(jnp.float32) + bias
        p = jax.nn.softmax(logits, axis=-1)
        w = p[:, :, 0] - lam * p[:, :, 1]
        return jnp.einsum('bhqk,bkhe->bqhe', w.astype(v.dtype), v)

    o = lax.map(block, (jnp.arange(nb), qb))
    o = jnp.moveaxis(o, 0, 1).reshape(B, S, ATT_HEADS, ATT_V_DIM)
    o = rmsnorm(o, head_g.reshape(ATT_HEADS, ATT_V_DIM)) * (1.0 - lambda_init)
    return o.reshape(B, S, ATT_WIDTH)


def mlstm_chunk_scan(q, k, v, ig, lf):
    B, H, S, d = q.shape
    L = MLSTM_CHUNK
    nc = S // L
    f32 = jnp.float32

    def chunks(a):
        return jnp.moveaxis(a.astype(f32).reshape(B, H, nc, L, *a.shape[3:]), 2, 0)

    causal = jnp.tril(jnp.ones((L, L), dtype=bool))

    def step(carry, inp):
        C, n, m = carry
        qc, kc, vc, ic, fc = inp
        b = jnp.cumsum(fc, axis=-1)
        D = jnp.where(causal, b[..., :, None] - b[..., None, :] + ic[..., None, :], -jnp.inf)
        inter = b + m[..., None]
        m_t = jnp.maximum(inter, jnp.max(D, axis=-1))
        Dw = jnp.exp(D - m_t[..., None])
        iw = jnp.exp(inter - m_t)
        s = jnp.einsum('bhtd,bhsd->bhts', qc, kc) * Dw
        num = iw[..., None] * jnp.einsum('bhed,bhtd->bhte', C, qc) + jnp.einsum('bhts,bhse->bhte', s, vc)
        den = iw * jnp.einsum('bhd,bhtd->bht', n, qc) + jnp.sum(s, axis=-1)
        h = num / jnp.maximum(jnp.abs(den), jnp.exp(-m_t))[..., None]
        bL = b[..., -1]
        g = bL[..., None] - b + ic
        m_new = jnp.maximum(bL + m, jnp.max(g, axis=-1))
        decay = jnp.exp(bL + m - m_new)
        wk = jnp.exp(g - m_new[..., None])
        C_new = decay[..., None, None] * C + jnp.einsum('bhs,bhse,bhsd->bhed', wk, vc, kc)
        n_new = decay[..., None] * n + jnp.einsum('bhs,bhsd->bhd', wk, kc)
        return (C_new, n_new, m_new), h

    init = (jnp.zeros((B, H, d, d), f32), jnp.zeros((B, H, d), f32), jnp.zeros((B, H), f32))
    _, hs = lax.scan(step, init, (chunks(q), chunks(k), chunks(v), chunks(ig), chunks(lf)))
    return jnp.moveaxis(hs, 0, 2).reshape(B, H, S, d)


def centred_conv(x, w, b):
    pad = CONV_WIDTH // 2
    S = x.shape[1]
    xp = jnp.pad(x, ((0, 0), (pad, pad), (0, 0)))
    out = b
    for j in range(CONV_WIDTH):
        out = out + xp[:, j:j + S] * w[j]
    return jax.nn.silu(out)


def mix_layer(h, w_in, gate_bias, conv_w, conv_b, rel_bias, lam, lambda_init,
              att_head_g, mlstm_head_g, w_out):
    B, S = h.shape[0], h.shape[1]
    proj = h @ w_in
    idx = list(np.cumsum(SPLIT_SIZES)[:-1])
    a_q, a_k, a_v, m_q, m_k, m_v, m_o, gates = jnp.split(proj, idx, axis=-1)
    a_q = a_q.reshape(B, S, ATT_HEADS, 2, ATT_QK_DIM)
    a_k = a_k.reshape(B, S, ATT_HEADS, 2, ATT_QK_DIM)
    a_v = a_v.reshape(B, S, ATT_HEADS, ATT_V_DIM)
    att_out = diff_attention(a_q, a_k, a_v, rel_bias, lam, lambda_init, att_head_g)
    qk = centred_conv(jnp.concatenate([m_q, m_k], axis=-1), conv_w, conv_b)
    m_q, m_k = jnp.split(qk, 2, axis=-1)

    def heads(a):
        return a.reshape(B, S, MLSTM_HEADS, MLSTM_HEAD_DIM).transpose(0, 2, 1, 3)

    mq = heads(m_q) * (MLSTM_HEAD_DIM ** -0.5)
    mk, mv = heads(m_k), heads(m_v)
    g = (gates.astype(jnp.float32) + gate_bias.astype(jnp.float32)).transpose(0, 2, 1)
    i_f, i_b, f_f, f_b = jnp.split(g, 4, axis=1)
    lf_f, lf_b = jax.nn.log_sigmoid(f_f), jax.nn.log_sigmoid(f_b)
    h_fwd = mlstm_chunk_scan(mq, mk, mv, i_f, lf_f)
    flip = lambda a: jnp.flip(a, axis=2)
    h_bwd = flip(mlstm_chunk_scan(flip(mq), flip(mk), flip(mv), flip(i_b), flip(lf_b)))
    hm = (h_fwd + h_bwd).transpose(0, 2, 1, 3)
    hm = rmsnorm(hm, mlstm_head_g.reshape(MLSTM_HEADS, MLSTM_HEAD_DIM))
    o_gate = jax.nn.sigmoid(m_o.astype(jnp.float32)).reshape(B, S, MLSTM_HEADS, MLSTM_HEAD_DIM)
    mlstm_out = (o_gate * hm).reshape(B, S, MLSTM_WIDTH).astype(h.dtype)
    return jnp.concatenate([att_out.astype(h.dtype), mlstm_out], axis=-1) @ w_out


def setup_inputs(seed: int = 0) -> dict:
    key = jax.random.key(seed)
    ks = jax.random.split(key, 32)
    f32 = jnp.float32
    nrm = lambda k, shape, s: jax.random.normal(k, shape, f32) * s
    gain = lambda k: 1.0 + nrm(k, (DEPTH, D_MODEL), 0.02)
    forget_bias = jnp.tile(jnp.linspace(3.0, 6.0, MLSTM_HEADS, dtype=f32), 2)
    gate_bias = jnp.concatenate([nrm(ks[10], (DEPTH, 2 * MLSTM_HEADS), 0.1),
                                 forget_bias[None, :] + nrm(ks[11], (DEPTH, 2 * MLSTM_HEADS), 0.1)], axis=-1)
    return {
        "x_prompt": nrm(ks[0], (BATCH, SEQ, D_MODEL), 1.0),
        "x_sample": nrm(ks[1], (DEC_BATCH, DEC_SEQ, D_MODEL), 1.0),
        "rel_bias": nrm(ks[2], (REL_BUCKETS, 2 * ATT_HEADS), 0.5),
        "ffn1_pre_g": gain(ks[3]),
        "ffn1_post_g": gain(ks[4]),
        "ffn1_w_gate": nrm(ks[5], (DEPTH, D_MODEL, FFN_DIM), D_MODEL ** -0.5),
        "ffn1_w_up": nrm(ks[6], (DEPTH, D_MODEL, FFN_DIM), D_MODEL ** -0.5),
        "ffn1_w_down": nrm(ks[7], (DEPTH, FFN_DIM, D_MODEL), FFN_DIM ** -0.5),
        "mix_pre_g": gain(ks[8]),
        "mix_post_g": gain(ks[9]),
        "w_in": nrm(ks[12], (DEPTH, D_MODEL, IN_WIDTH), D_MODEL ** -0.5),
        "gate_bias": gate_bias,
        "conv_w": nrm(ks[13], (DEPTH, CONV_WIDTH, 2 * MLSTM_WIDTH), CONV_WIDTH ** -0.5),
        "conv_b": nrm(ks[14], (DEPTH, 2 * MLSTM_WIDTH), 0.01),
        "lambda_q1": nrm(ks[15], (DEPTH, ATT_QK_DIM), 0.1),
        "lambda_k1": nrm(ks[16], (DEPTH, ATT_QK_DIM), 0.1),
        "lambda_q2": nrm(ks[17], (DEPTH, ATT_QK_DIM), 0.1),
        "lambda_k2": nrm(ks[18], (DEPTH, ATT_QK_DIM), 0.1),
        "att_head_g": 1.0 + nrm(ks[19], (DEPTH, ATT_WIDTH), 0.02),
        "mlstm_head_g": 1.0 + nrm(ks[20], (DEPTH, MLSTM_WIDTH), 0.02),
        "w_out": nrm(ks[21], (DEPTH, MIX_WIDTH, D_MODEL), MIX_WIDTH ** -0.5),
        "ffn2_pre_g": gain(ks[22]),
        "ffn2_post_g": gain(ks[23]),
        "ffn2_w_gate": nrm(ks[24], (DEPTH, D_MODEL, FFN_DIM), D_MODEL ** -0.5),
        "ffn2_w_up": nrm(ks[25], (DEPTH, D_MODEL, FFN_DIM), D_MODEL ** -0.5),
        "ffn2_w_down": nrm(ks[26], (DEPTH, FFN_DIM, D_MODEL), FFN_DIM ** -0.5),
    }


def reference(x_prompt, x_sample, rel_bias,
              ffn1_pre_g, ffn1_post_g, ffn1_w_gate, ffn1_w_up, ffn1_w_down,
              mix_pre_g, mix_post_g, w_in, gate_bias, conv_w, conv_b,
              lambda_q1, lambda_k1, lambda_q2, lambda_k2,
              att_head_g, mlstm_head_g, w_out,
              ffn2_pre_g, ffn2_post_g, ffn2_w_gate, ffn2_w_up, ffn2_w_down):
    def trunk(x):
        for l in range(DEPTH):
            lambda_init = 0.8 - 0.6 * math.exp(-0.3 * l)
            lam = (jnp.exp(jnp.sum(lambda_q1[l].astype(jnp.float32) * lambda_k1[l].astype(jnp.float32)))
                   - jnp.exp(jnp.sum(lambda_q2[l].astype(jnp.float32) * lambda_k2[l].astype(jnp.float32)))
                   + lambda_init)
            h = rmsnorm(x, ffn1_pre_g[l])
            x = x + 0.5 * rmsnorm(swiglu(h, ffn1_w_gate[l], ffn1_w_up[l], ffn1_w_down[l]), ffn1_post_g[l])
            h = rmsnorm(x, mix_pre_g[l])
            mixed = mix_layer(h, w_in[l], gate_bias[l], conv_w[l], conv_b[l], rel_bias, lam,
                              lambda_init, att_head_g[l], mlstm_head_g[l], w_out[l])
            x = x + rmsnorm(mixed, mix_post_g[l])
            h = rmsnorm(x, ffn2_pre_g[l])
            x = x + 0.5 * rmsnorm(swiglu(h, ffn2_w_gate[l], ffn2_w_up[l], ffn2_w_down[l]), ffn2_post_g[l])
        return x

    y_prompt = trunk(x_prompt)
    y_sample = trunk(x_sample)
    return (y_prompt, y_sample)
```

```python
import math
from contextlib import ExitStack

import numpy as np
import ml_dtypes

import concourse.bass as bass
import concourse.mybir as mybir
from concourse.bass_utils import run_bass_kernel_spmd

F32 = mybir.dt.float32
BF16 = mybir.dt.bfloat16
AF = mybir.ActivationFunctionType
ALU = mybir.AluOpType
AX = mybir.AxisListType

S = 4096
D = 2048
FF = 5632
NT = 8
EPS = 1e-6
LAMBDA_INIT = 0.8 - 0.6 * math.exp(-0.3 * 0)


class Buf:
    __slots__ = ("name", "w", "r")

    def __init__(self, name):
        self.name = name
        self.w = None
        self.r = {}


class Op:
    __slots__ = ("idx", "eng", "emit", "deps", "is_dma", "stream", "count", "signal")

    def __init__(self, idx, eng, emit, is_dma, stream):
        self.idx = idx
        self.eng = eng
        self.emit = emit
        self.deps = []
        self.is_dma = is_dma
        self.stream = stream
        self.count = 0
        self.signal = False


class Prog:
    ENGS = ("pe", "act", "dve", "pool", "sp")
    SAME_ENGINE_SYNC = {"pe": False, "act": True, "dve": True, "pool": True, "sp": False}

    def __init__(self):
        self.ops = []
        self.by_eng = {e: [] for e in self.ENGS}
        self.streams = {}
        self.nbuf = 0
        self.barrier_op = None
        self.barrier_seen = set()
        self.tokens = {}

    def buf(self, name=None):
        self.nbuf += 1
        return Buf(name or f"b{self.nbuf}")

    def bufs(self, n, name="b"):
        return [self.buf(f"{name}{i}") for i in range(n)]

    @staticmethod
    def _key(op):
        return ("d", op.stream) if op.is_dma else ("e", op.eng)

    def _add(self, eng, emit, reads, writes, is_dma, stream):
        op = Op(len(self.ops), eng, emit, is_dma, stream)
        deps = {}
        for b in reads:
            if b.w is not None:
                deps[b.w.idx] = b.w
        for b in writes:
            if b.w is not None:
                deps[b.w.idx] = b.w
            for r in b.r.values():
                deps[r.idx] = r
        if self.barrier_op is not None and eng not in self.barrier_seen:
            deps[self.barrier_op.idx] = self.barrier_op
            self.barrier_seen.add(eng)
        op.deps = list(deps.values())
        k = self._key(op)
        for b in reads:
            b.r[k] = op
        for b in writes:
            b.w = op
            b.r = {}
        self.ops.append(op)
        self.by_eng[eng].append(op)
        if is_dma:
            self.streams.setdefault(stream, []).append(op)
        return op

    def op(self, eng, emit, reads=(), writes=()):
        return self._add(eng, emit, reads, writes, False, None)

    def dma(self, eng, stream, emit, reads=(), writes=()):
        tok = self.tokens.get(stream)
        if tok is None:
            tok = self.tokens[stream] = Buf('tok_' + stream)
        return self._add(eng, emit, reads, list(writes) + [tok], True, stream)

    def barrier(self, scratch_ap):
        deps = []
        for e in self.ENGS:
            for o in reversed(self.by_eng[e]):
                if not o.is_dma:
                    deps.append(o)
                    break
        for s, l in self.streams.items():
            deps.append(l[-1])
        op = Op(len(self.ops), "pool", lambda e: e.memset(scratch_ap, 0.0), False, None)
        op.deps = deps
        self.ops.append(op)
        self.by_eng["pool"].append(op)
        self.barrier_op = op
        self.barrier_seen = {"pool"}
        return op

    def emit_all(self, nc, final_wait_ops=()):
        def need(op, d):
            if d.is_dma:
                return True
            if d.eng == op.eng and not op.is_dma and not self.SAME_ENGINE_SYNC[d.eng]:
                return False
            return True

        for op in self.ops:
            for d in op.deps:
                if not d.is_dma and need(op, d):
                    d.signal = True
        for d in final_wait_ops:
            if not d.is_dma:
                d.signal = True
        es = ExitStack()
        sems = {e: es.enter_context(nc.semaphore(f"s_{e}")) for e in self.ENGS}
        ssem = {s: es.enter_context(nc.semaphore(f"d_{s}")) for s in self.streams}
        for e in self.ENGS:
            c = 0
            for op in self.by_eng[e]:
                if op.is_dma:
                    continue
                if op.signal:
                    c += 1
                op.count = c
        for s, lst in self.streams.items():
            for i, op in enumerate(lst):
                op.count = 16 * (i + 1)
        self.stats = {e: len(self.by_eng[e]) for e in self.ENGS}

        def waits_for(op, seen):
            best = {}
            for d in op.deps:
                if not need(op, d):
                    continue
                key = self._key(d)
                sem = ssem[d.stream] if d.is_dma else sems[d.eng]
                if seen.get(key, 0) >= d.count:
                    continue
                if key not in best or best[key][1] < d.count:
                    best[key] = (sem, d.count)
            for key, (sem, val) in best.items():
                seen[key] = val
            return list(best.values())

        block = es.enter_context(nc.Block())

        def run_engine(engname):
            def f(eng):
                seen = {}
                for op in self.by_eng[engname]:
                    for sem, val in waits_for(op, seen):
                        eng.wait_ge(sem, val)
                    ins = op.emit(eng)
                    if op.is_dma:
                        ins.then_inc(ssem[op.stream], 16)
                    elif op.signal:
                        ins.then_inc(sems[engname], 1)
                if engname == "sp":
                    for d in final_wait_ops:
                        eng.wait_ge(ssem[d.stream] if d.is_dma else sems[d.eng], d.count)
            return f

        block.tensor(run_engine("pe"))
        block.scalar(run_engine("act"))
        block.vector(run_engine("dve"))
        block.gpsimd(run_engine("pool"))
        block.sync(run_engine("sp"))
        es.close()


class Arena:
    def __init__(self, ap):
        self.ap = ap
        self.off = 0
        self.W = ap.shape[1]

    def _take(self, nbytes):
        w = ((nbytes + 31) // 32) * 8
        s = self.off
        self.off += w
        assert self.off <= self.W, f"SBUF arena overflow {self.off * 4} > {self.W * 4}"
        return s

    @staticmethod
    def _shape(v, shape):
        if len(shape) == 1:
            return v
        if len(shape) == 2:
            return v.rearrange("p (a b) -> p a b", a=shape[0])
        if len(shape) == 3:
            return v.rearrange("p (a b c) -> p a b c", a=shape[0], b=shape[1])
        raise ValueError

    def f32(self, *shape):
        n = int(np.prod(shape))
        s = self._take(4 * n)
        return self._shape(self.ap[:, s:s + n], shape)

    def bf16(self, *shape):
        n = int(np.prod(shape))
        s = self._take(2 * n)
        return self._shape(self.ap[:, s:s + (n + 1) // 2].bitcast(BF16)[:, :n], shape)


def rel_bucket_np(rel):
    nb = 16
    max_exact = 8
    ret = np.where(rel > 0, nb, 0)
    n = np.abs(rel)
    nf = np.maximum(n, 1).astype(np.float32)
    large = max_exact + (np.log(nf / np.float32(max_exact)) / np.float32(math.log(128 / max_exact))
                         * np.float32(nb - max_exact)).astype(np.int32)
    large = np.minimum(large, nb - 1)
    return ret + np.where(n < max_exact, n, large)


def host_consts():
    c = {}
    c["ident"] = np.eye(128, dtype=np.float32).astype(ml_dtypes.bfloat16)
    s_, t_ = np.meshgrid(np.arange(128), np.arange(128), indexing="ij")
    c["trif"] = (s_ <= t_).astype(np.float32)
    c["trib"] = (s_ >= t_).astype(np.float32)
    c["ones"] = np.ones((128, 128), np.float32)
    try:
        import jax
        import jax.numpy as jnp
        with jax.default_device(jax.devices("cpu")[0]):
            rel = jnp.arange(-255, 256)
            nb = 16
            max_exact = 8
            ret = jnp.where(rel > 0, nb, 0)
            n = jnp.abs(rel)
            nf = jnp.maximum(n, 1).astype(jnp.float32)
            large = max_exact + (jnp.log(nf / max_exact) / math.log(128 / max_exact) * (nb - max_exact)).astype(jnp.int32)
            large = jnp.minimum(large, nb - 1)
            bk = np.asarray(ret + jnp.where(n < max_exact, n, large))
    except Exception:
        bk = rel_bucket_np(np.arange(-255, 256))
    oh = np.zeros((32, 512), np.float32)
    oh[bk, np.arange(511)] = 1.0
    c["oh"] = oh
    return c


def build_program(dbg=False, phases="PABCD"):
    nc = bass.Bass("TRN2", target_bir_lowering=False)
    P = Prog()
    es = ExitStack()

    def din(name, shape, dt=F32):
        return nc.dram_tensor(name, list(shape), dt, kind="ExternalInput").ap()

    def scr(name, shape, dt):
        return nc.dram_tensor(name, list(shape), dt, kind=("ExternalOutput" if dbg else "Internal")).ap()

    x = din("x", [S, D])
    relb = din("rel_bias", [32, 16])
    g_names = ["ffn1_pre_g", "ffn1_post_g", "mix_pre_g", "mix_post_g", "ffn2_pre_g", "ffn2_post_g"]
    gv = {n: din(n, [1, D]) for n in g_names}
    w1g = din("ffn1_w_gate", [D, FF]); w1u = din("ffn1_w_up", [D, FF]); w1d = din("ffn1_w_down", [FF, D])
    w2g = din("ffn2_w_gate", [D, FF]); w2u = din("ffn2_w_up", [D, FF]); w2d = din("ffn2_w_down", [FF, D])
    w_in = din("w_in", [D, 7184]); w_out = din("w_out", [D, D])
    gate_bias = din("gate_bias", [1, 16]); conv_w = din("conv_w", [5, 2048]); conv_b = din("conv_b", [1, 2048])
    lq1 = din("lambda_q1", [1, 64]); lk1 = din("lambda_k1", [1, 64]); lq2 = din("lambda_q2", [1, 64]); lk2 = din("lambda_k2", [1, 64])
    att_g = din("att_head_g", [1, 1024]); ml_g = din("mlstm_head_g", [1, 1024])
    c_ident = din("ident", [128, 128], BF16); c_trif = din("trif", [128, 128]); c_trib = din("trib", [128, 128])
    c_ones = din("ones", [128, 128]); c_oh = din("oh", [32, 512])
    y = nc.dram_tensor("y", [S, D], F32, kind="ExternalOutput").ap()

    sw1g = scr("sw1g", [22, 128, 16, 256], BF16); sw1u = scr("sw1u", [22, 128, 16, 256], BF16)
    sw2g = scr("sw2g", [22, 128, 16, 256], BF16); sw2u = scr("sw2u", [22, 128, 16, 256], BF16)
    sw1d = scr("sw1d", [4, 128, 44, 512], BF16); sw2d = scr("sw2d", [4, 128, 44, 512], BF16)
    swin = scr("swin", [28, 128, 16, 256], BF16); swgt = scr("swgt", [128, 16, 16], BF16)
    swout = scr("swout", [8, 128, 16, 256], BF16)
    X1 = scr("X1", [S, D], F32); X2 = scr("X2", [S, D], F32)
    QT = scr("QT", [8, 128, S], BF16); KT = scr("KT", [8, 128, S], BF16)
    MQ = scr("MQ", [8, 128, S], BF16); MK = scr("MK", [8, 128, S], BF16)
    VV = scr("VV", [S, 1024], BF16); MV = scr("MV", [S, 1024], BF16); MO = scr("MO", [S, 1024], BF16)
    GATES = scr("GATES", [S, 16], F32); CONCAT = scr("CONCAT", [S, 2048], BF16)
    FSCR = scr("FSCR", [16, 512], F32)

    arena_t = es.enter_context(nc.sbuf_tensor("arena", [128, 53000], F32))
    ps = es.enter_context(nc.psum_tensor("ps", [128, 8, 512], F32))
    bps = P.bufs(8, "ps")

    def bank(b):
        return ps[:, b, :]

    def bankbf(b):
        return ps[:, b, :].bitcast(BF16)

    AR = Arena(arena_t[:])
    rrs = {"c": 0, "wc": 0}

    def cs():
        rrs["c"] += 1
        return f"c{rrs['c'] % 4}"

    def wcs():
        rrs["wc"] += 1
        return f"wc{rrs['wc'] % 8}"

    ident = AR.bf16(128); b_ident = P.buf("ident")
    barscr = AR.f32(8)
    P.dma("sp", cs(), lambda e: e.dma_start(out=ident, in_=c_ident), writes=[b_ident])
    persist_off = AR.off

    def bcast_row(ap_row, n, off=0):
        return bass.AP(ap_row.tensor, ap_row.offset + off, [[0, 128], [1, n]])

    def load_gT(dst, g_ap, b):
        src = g_ap.rearrange("o (c p) -> p (o c)", p=128)
        P.dma("sp", cs(), lambda e: e.dma_start(out=dst, in_=src, allow_slow_non_contiguous=True), writes=[b])

    final_ops = []
    rr = {"bank": 0, "eng": 0}

    def next_bank():
        b = rr["bank"]
        rr["bank"] = (b + 1) % 4
        return b

    def alt_eng():
        rr["eng"] ^= 1
        return "act" if rr["eng"] else "dve"

    def copy_op(eng, out, in_, reads, writes):
        if eng == "act":
            P.op("act", lambda e: e.activation(out=out, in_=in_, func=AF.Copy), reads=reads, writes=writes)
        else:
            P.op(eng, lambda e: e.tensor_copy(out=out, in_=in_), reads=reads, writes=writes)

    PRE = []

    pre_gate = []

    def cast_dma(out, in_, wbuf):
        PRE.append(lambda: P.dma("pool", wcs(), lambda e: e.dma_start(out=out, in_=in_), reads=list(pre_gate), writes=[wbuf]))

    def cast_g256(w_ap, col0, ng, dst, name):
        wv = w_ap[:, col0:col0 + ng * 256].rearrange("(c p) (g j) -> g p c j", p=128, j=256)
        bs = P.bufs(ng, name)
        for g in range(ng):
            cast_dma(dst[g], wv[g], bs[g])
        return bs

    def cast_wd(w_ap, dst, name):
        wv = w_ap.rearrange("(f p) (n j) -> n p f j", p=128, j=512)
        bs = [P.bufs(4, f"{name}{n}_") for n in range(4)]
        for n in range(4):
            for fp in range(4):
                cast_dma(dst[n][:, fp * 11:(fp + 1) * 11, :], wv[n][:, fp * 11:(fp + 1) * 11, :], bs[n][fp])
        return bs

    wv1g = w1g.rearrange("(c p) (g j) -> g p c j", p=128, j=256)
    wv1u = w1u.rearrange("(c p) (g j) -> g p c j", p=128, j=256)
    b_w1g = P.bufs(22, "w1g"); b_w1u = P.bufs(22, "w1u")
    for g in range(22):
        cast_dma(sw1g[g], wv1g[g], b_w1g[g])
        cast_dma(sw1u[g], wv1u[g], b_w1u[g])
    b_w1d = cast_wd(w1d, sw1d, "w1d")
    b_win = cast_g256(w_in, 0, 28, swin, "win")
    b_wgt = P.buf("wgt")
    wgt_v = w_in[:, 7168:7184].rearrange("(c p) j -> p c j", p=128)
    cast_dma(swgt, wgt_v, b_wgt)
    n_pre_first = len(PRE)
    b_wout = cast_g256(w_out, 0, 8, swout, "wout")
    b_w2g = cast_g256(w2g, 0, 22, sw2g, "w2g"); b_w2u = cast_g256(w2u, 0, 22, sw2u, "w2u")
    b_w2d = cast_wd(w2d, sw2d, "w2d")
    if "A" not in phases:
        for f_ in PRE[:n_pre_first]:
            f_()
    PRE_REST = PRE[n_pre_first:]

    def emit_pre_slice(i, n):
        k = (len(PRE_REST) + n - 1) // n
        for f_ in PRE_REST[i * k:(i + 1) * k]:
            f_()

    b_X1 = P.bufs(NT * 4, "X1"); b_X2 = P.bufs(NT * 4, "X2")
    b_QT = P.bufs(8, "QT"); b_KT = P.bufs(8, "KT"); b_MQ = P.bufs(8, "MQ"); b_MK = P.bufs(8, "MK")
    b_VV = P.buf("VV"); b_MV = P.buf("MV"); b_MO = P.buf("MO"); b_GA = P.buf("GA"); b_CC = P.buf("CC")

    def ffn_phase_alloc(ngb):
        A = {}
        A["xy"] = AR.f32(4, 2048); A["b_xy"] = P.bufs(4, "xy")
        A["hT"] = AR.bf16(16, 512); A["b_hT"] = P.buf("hT")
        A["actT"] = AR.bf16(44, 512); A["b_act"] = P.bufs(44, "act")
        A["wgs"] = [AR.bf16(16, 256) for _ in range(2)]; A["b_wgs"] = P.bufs(2, "wgs")
        A["wus"] = [AR.bf16(16, 256) for _ in range(2)]; A["b_wus"] = P.bufs(2, "wus")
        A["wds"] = [AR.bf16(11, 512) for _ in range(2)]; A["b_wds"] = P.bufs(2, "wds")
        A["hn"] = [AR.bf16(2048) for _ in range(2)]; A["b_hn"] = P.bufs(2, "hn")
        A["sg"] = [AR.f32(512) for _ in range(2)]; A["b_sg"] = P.bufs(2, "sg")
        A["xr"] = [AR.f32(2048) for _ in range(2)]; A["b_xr"] = P.bufs(2, "xr")
        A["gb"] = [AR.f32(2048) for _ in range(ngb)]; A["b_gb"] = P.bufs(ngb, "gb")
        A["stg"] = [AR.bf16(512) for _ in range(3)]; A["b_stg"] = P.bufs(3, "stg")
        A["stgT"] = [AR.bf16(4, 256) for _ in range(2)]; A["b_stgT"] = P.bufs(2, "stgT")
        A["ss"] = AR.f32(4); A["b_ss"] = P.bufs(2, "ss")
        A["rstd"] = AR.f32(4); A["b_rstd"] = P.bufs(2, "rstd")
        A["gT"] = [AR.f32(16) for _ in range(2)]; A["b_gT"] = P.bufs(2, "gT")
        A["cnt"] = {"wgu": 0, "wds": 0, "stg": 0, "stgT": 0, "xr": 0, "sg": 0, "hn": 0}
        return A

    def stats_rstd(A, n_feat, const, hf):
        ss, rstd = A["ss"][:, 2 * hf:2 * hf + 2], A["rstd"][:, 2 * hf:2 * hf + 2]
        P.op("dve", lambda e: e.tensor_scalar(out=rstd, in0=ss, scalar1=1.0 / n_feat, scalar2=EPS, op0=ALU.mult, op1=ALU.add),
             reads=[A["b_ss"][hf]], writes=[A["b_rstd"][hf]])
        P.op("act", lambda e: e.activation(out=rstd, in_=rstd, func=AF.Ln), reads=[A["b_rstd"][hf]], writes=[A["b_rstd"][hf]])
        P.op("act", lambda e: e.activation(out=rstd, in_=rstd, func=AF.Exp, scale=-0.5, bias=float(math.log(const))),
             reads=[A["b_rstd"][hf]], writes=[A["b_rstd"][hf]])

    def sumsq_blocks(A, hf):
        xy = A["xy"]
        for b in (2 * hf, 2 * hf + 1):
            j = A["cnt"]["hn"] % 2
            A["cnt"]["hn"] += 1
            P.op("act", lambda e, b=b, j=j: e.activation(out=A["hn"][j], in_=xy[:, b, :], func=AF.Square, accum_out=A["ss"][:, b:b + 1]),
                 reads=[A["b_xy"][b]], writes=[A["b_hn"][j], A["b_ss"][hf]])

    def norm_transpose(A, gT, b_gT, src=None, b_src=None, scale=True, blocks=(0, 1, 2, 3)):
        hT = A["hT"]
        for b in blocks:
            if scale:
                j = A["cnt"]["hn"] % 2
                A["cnt"]["hn"] += 1
                hn, b_hn = A["hn"][j], A["b_hn"][j]
                P.op("act", lambda e, b=b, hn=hn: e.activation(out=hn, in_=A["xy"][:, b, :], func=AF.Copy, scale=A["rstd"][:, b:b + 1]),
                     reads=[A["b_xy"][b], A["b_rstd"][b // 2]], writes=[b_hn])
            else:
                hn, b_hn = src[:, b, :], b_src[b]
            for half in range(2):
                bk = next_bank()
                pb = bankbf(bk)
                for j8 in range(8):
                    c = half * 8 + j8
                    P.op("pe", lambda e, pb=pb, j8=j8, c=c, hn=hn: e.transpose(out=pb[:, j8 * 128:(j8 + 1) * 128], in_=hn[:, c * 128:(c + 1) * 128], identity=ident),
                         reads=[b_hn, b_ident], writes=[bps[bk]])
                dst = hT[:, half * 8:(half + 1) * 8, b * 128:(b + 1) * 128]
                srcp = pb[:, 0:1024].rearrange("p (a b) -> p a b", a=8)
                if gT is not None:
                    gbc = gT[:, half * 8:(half + 1) * 8].unsqueeze(2).to_broadcast([128, 8, 128])
                    P.op("dve", lambda e, dst=dst, srcp=srcp, gbc=gbc: e.tensor_tensor(out=dst, in0=srcp, in1=gbc, op=ALU.mult),
                         reads=[bps[bk], b_gT], writes=[A["b_hT"]])
                else:
                    copy_op(alt_eng(), dst, srcp, [bps[bk]], [A["b_hT"]])

    def ffn(A, swg, swu, swd, bwg, bwu, bwd):
        hT, actT = A["hT"], A["actT"]
        for g in range(22):
            s = A["cnt"]["wgu"] % 2
            A["cnt"]["wgu"] += 1
            wg_t, wu_t = A["wgs"][s], A["wus"][s]
            P.dma("sp", f"wg{s}", lambda e, g=g, wg_t=wg_t: e.dma_start(out=wg_t, in_=swg[g]), reads=[bwg[g]], writes=[A["b_wgs"][s]])
            P.dma("sp", f"wu{s}", lambda e, g=g, wu_t=wu_t: e.dma_start(out=wu_t, in_=swu[g]), reads=[bwu[g]], writes=[A["b_wus"][s]])
            for j in range(2):
                f = 2 * g + j
                bg, bu = next_bank(), next_bank()
                for c in range(16):
                    P.op("pe", lambda e, bg=bg, c=c, j=j, wg_t=wg_t: e.matmul(bank(bg), lhsT=wg_t[:, c, j * 128:(j + 1) * 128], rhs=hT[:, c, :], start=(c == 0), stop=(c == 15)),
                         reads=[A["b_wgs"][s], A["b_hT"]], writes=[bps[bg]])
                for c in range(16):
                    P.op("pe", lambda e, bu=bu, c=c, j=j, wu_t=wu_t: e.matmul(bank(bu), lhsT=wu_t[:, c, j * 128:(j + 1) * 128], rhs=hT[:, c, :], start=(c == 0), stop=(c == 15)),
                         reads=[A["b_wus"][s], A["b_hT"]], writes=[bps[bu]])
                k = A["cnt"]["sg"] % 2
                A["cnt"]["sg"] += 1
                sg = A["sg"][k]
                P.op("act", lambda e, bg=bg, sg=sg: e.activation(out=sg, in_=bank(bg), func=AF.Silu), reads=[bps[bg]], writes=[A["b_sg"][k]])
                P.op("dve", lambda e, bu=bu, sg=sg, f=f: e.tensor_tensor(out=actT[:, f, :], in0=bank(bu), in1=sg, op=ALU.mult),
                     reads=[bps[bu], A["b_sg"][k]], writes=[A["b_act"][f]])
        for n in range(4):
            for fp in range(4):
                s = A["cnt"]["wds"] % 2
                A["cnt"]["wds"] += 1
                wd_t = A["wds"][s]
                P.dma("sp", f"wd{s}", lambda e, n=n, fp=fp, wd_t=wd_t: e.dma_start(out=wd_t, in_=swd[n][:, fp * 11:(fp + 1) * 11, :]),
                      reads=[bwd[n][fp]], writes=[A["b_wds"][s]])
                for fi in range(11):
                    f = fp * 11 + fi
                    for b in range(4):
                        P.op("pe", lambda e, b=b, f=f, fi=fi, wd_t=wd_t: e.matmul(bank(4 + b), lhsT=actT[:, f, b * 128:(b + 1) * 128], rhs=wd_t[:, fi, :], start=(f == 0), stop=(f == 43)),
                             reads=[A["b_act"][f], A["b_wds"][s]], writes=[bps[4 + b]])
            for b in range(4):
                copy_op(alt_eng(), A["xy"][:, b, n * 512:(n + 1) * 512], bank(4 + b), [bps[4 + b]], [A["b_xy"][b]])

    def post_residual(A, t, res_dram, b_res, gb, b_gb, const, dst_dram, b_dst, is_final, hf):
        xy = A["xy"]
        sumsq_blocks(A, hf)
        stats_rstd(A, D, const, hf)
        for b in (2 * hf, 2 * hf + 1):
            k = A["cnt"]["xr"] % 2
            A["cnt"]["xr"] += 1
            xr = A["xr"][k]
            r0 = t * 512 + b * 128
            rd = [b_res[t * 4 + b]] if b_res is not None else []
            P.dma("sp", f"xr{k}", lambda e, xr=xr, r0=r0: e.dma_start(out=xr, in_=res_dram[r0:r0 + 128, :]), reads=rd, writes=[A["b_xr"][k]])
            P.op("dve", lambda e, b=b: e.scalar_tensor_tensor(out=xy[:, b, :], in0=xy[:, b, :], scalar=A["rstd"][:, b:b + 1], in1=gb, op0=ALU.mult, op1=ALU.mult),
                 reads=[A["b_xy"][b], A["b_rstd"][hf], b_gb], writes=[A["b_xy"][b]])
            P.op("dve", lambda e, b=b, xr=xr: e.tensor_tensor(out=xy[:, b, :], in0=xy[:, b, :], in1=xr, op=ALU.add),
                 reads=[A["b_xy"][b], A["b_xr"][k]], writes=[A["b_xy"][b]])
            wr = [b_dst[t * 4 + b]] if b_dst is not None else []
            op = P.dma("pool", f"sx{b}", lambda e, b=b, r0=r0: e.dma_start(out=dst_dram[r0:r0 + 128, :], in_=xy[:, b, :]), reads=[A["b_xy"][b]], writes=wr)
            if is_final:
                final_ops.append(op)

    if "A" in phases:
        AR.off = persist_off
        A = ffn_phase_alloc(1)
        wgt = AR.bf16(16, 16); b_wgtt = P.buf("wgtt")
        gbias = AR.f32(16); b_gbias = P.buf("gbias")
        gst = AR.f32(4, 16); b_gst = P.buf("gst")
        load_gT(A["gT"][0], gv["ffn1_pre_g"], A["b_gT"][0])
        load_gT(A["gT"][1], gv["mix_pre_g"], A["b_gT"][1])
        P.dma("sp", cs(), lambda e: e.dma_start(out=A["gb"][0], in_=bcast_row(gv["ffn1_post_g"], D)), writes=[A["b_gb"][0]])
        P.dma("sp", cs(), lambda e: e.dma_start(out=gbias, in_=bcast_row(gate_bias, 16)), writes=[b_gbias])
        FM = {}
        for g in range(4):
            FM[g] = (QT, b_QT, 2 * g); FM[4 + g] = (KT, b_KT, 2 * g)
            FM[12 + g] = (MQ, b_MQ, 2 * g); FM[16 + g] = (MK, b_MK, 2 * g)
        TM = {}
        for g in range(4):
            TM[8 + g] = (VV, b_VV, g * 256); TM[20 + g] = (MV, b_MV, g * 256); TM[24 + g] = (MO, b_MO, g * 256)
        for t in range(NT):
            xy = A["xy"]
            xsrc = x[t * 512:(t + 1) * 512, :].rearrange("(b p) d -> p b d", p=128)
            P.dma("sp", "xl", lambda e, xsrc=xsrc: e.dma_start(out=xy, in_=xsrc), writes=A["b_xy"])
            if t == 0:
                pre_gate.extend(A["b_xy"])
                for f_ in PRE[:8]:
                    f_()
                del pre_gate[:]
                for f_ in PRE[8:n_pre_first]:
                    f_()
            for hf in range(2):
                sumsq_blocks(A, hf)
                stats_rstd(A, D, 1.0, hf)
                norm_transpose(A, A["gT"][0], A["b_gT"][0], blocks=(2 * hf, 2 * hf + 1))
            ffn(A, sw1g, sw1u, sw1d, b_w1g, b_w1u, b_w1d)
            for hf in range(2):
                post_residual(A, t, x, None, A["gb"][0], A["b_gb"][0], 0.5, X1, b_X1, False, hf)
                sumsq_blocks(A, hf)
                stats_rstd(A, D, 1.0, hf)
                norm_transpose(A, A["gT"][1], A["b_gT"][1], blocks=(2 * hf, 2 * hf + 1))
            hT = A["hT"]
            for g in range(28):
                s = A["cnt"]["wgu"] % 2
                A["cnt"]["wgu"] += 1
                w_t = A["wgs"][s]
                P.dma("sp", f"wg{s}", lambda e, g=g, w_t=w_t: e.dma_start(out=w_t, in_=swin[g]), reads=[b_win[g]], writes=[A["b_wgs"][s]])
                if g in FM:
                    dst, bdst, ch0 = FM[g]
                    for j in range(2):
                        bk = next_bank()
                        for c in range(16):
                            P.op("pe", lambda e, bk=bk, c=c, j=j, w_t=w_t: e.matmul(bank(bk), lhsT=w_t[:, c, j * 128:(j + 1) * 128], rhs=hT[:, c, :], start=(c == 0), stop=(c == 15)),
                                 reads=[A["b_wgs"][s], A["b_hT"]], writes=[bps[bk]])
                        k = A["cnt"]["stg"] % 3
                        A["cnt"]["stg"] += 1
                        stg = A["stg"][k]
                        copy_op(alt_eng(), stg, bank(bk), [bps[bk]], [A["b_stg"][k]])
                        ch = ch0 + j
                        P.dma("pool", f"sg{k}", lambda e, dst=dst, ch=ch, stg=stg, t=t: e.dma_start(out=dst[ch][:, t * 512:(t + 1) * 512], in_=stg),
                              reads=[A["b_stg"][k]], writes=[bdst[ch]])
                else:
                    dst, bdst, col0 = TM[g]
                    k = A["cnt"]["stgT"] % 2
                    A["cnt"]["stgT"] += 1
                    stgT = A["stgT"][k]
                    for b in range(4):
                        bk = next_bank()
                        for c in range(16):
                            P.op("pe", lambda e, bk=bk, c=c, b=b, w_t=w_t: e.matmul(bank(bk)[:, 0:256], lhsT=hT[:, c, b * 128:(b + 1) * 128], rhs=w_t[:, c, :], start=(c == 0), stop=(c == 15)),
                                 reads=[A["b_wgs"][s], A["b_hT"]], writes=[bps[bk]])
                        copy_op(alt_eng(), stgT[:, b, :], bank(bk)[:, 0:256], [bps[bk]], [A["b_stgT"][k]])
                    dv = dst[t * 512:(t + 1) * 512, col0:col0 + 256].rearrange("(b p) j -> p b j", p=128)
                    P.dma("pool", f"sT{k}", lambda e, dv=dv, stgT=stgT: e.dma_start(out=dv, in_=stgT), reads=[A["b_stgT"][k]], writes=[bdst])
            if t == 0:
                P.dma("sp", cs(), lambda e: e.dma_start(out=wgt, in_=swgt), reads=[b_wgt], writes=[b_wgtt])
            for b in range(4):
                bk = next_bank()
                for c in range(16):
                    P.op("pe", lambda e, bk=bk, c=c, b=b: e.matmul(bank(bk)[:, 0:16], lhsT=hT[:, c, b * 128:(b + 1) * 128], rhs=wgt[:, c, :], start=(c == 0), stop=(c == 15)),
                         reads=[b_wgtt, A["b_hT"]], writes=[bps[bk]])
                P.op("dve", lambda e, bk=bk, b=b: e.tensor_tensor(out=gst[:, b, :], in0=bank(bk)[:, 0:16], in1=gbias, op=ALU.add),
                     reads=[bps[bk], b_gbias], writes=[b_gst])
            gdv = GATES[t * 512:(t + 1) * 512, :].rearrange("(b p) j -> p b j", p=128)
            P.dma("pool", "sG", lambda e, gdv=gdv: e.dma_start(out=gdv, in_=gst), reads=[b_gst], writes=[b_GA])
            emit_pre_slice(t, NT)
        P.barrier(barscr)

    if "A" not in phases:
        emit_pre_slice(0, 1)

    if "B" in phases:
        AR.off = persist_off
        qT = [AR.bf16(S) for _ in range(2)]; b_qT = P.bufs(2, "qT")
        kT = [AR.bf16(S) for _ in range(2)]; b_kT = P.bufs(2, "kT")
        vA = [AR.bf16(32, 129) for _ in range(2)]; b_vA = P.bufs(2, "vA")
        ET = AR.f32(48, 128); b_ET = P.buf("ET")
        HR = AR.f32(48, 128); b_HR = P.buf("HR")
        cm = AR.f32(16); cp = AR.f32(16); b_cc = P.buf("cmcp")
        agb = AR.f32(1024); b_agb = P.buf("agb")
        lam4 = AR.f32(4, 64); b_lam4 = P.buf("lam4")
        lprod = AR.f32(2, 64); lsum = AR.f32(2); nlam = AR.f32(1); b_lam = P.buf("lam")
        pT = [AR.bf16(2, 512) for _ in range(3)]; b_pT = P.bufs(3, "pT")
        ocp = AR.f32(4, 129); b_ocp = P.buf("ocp")
        vP = [[AR.bf16(32, 129) for _ in range(2)] for _ in range(2)]; b_vP = [P.bufs(2, f"vP{i}") for i in range(2)]
        cdm = AR.f32(16); ecd = AR.f32(16); b_ecd = P.buf("ecd")
        ofall2 = [AR.f32(32, 128) for _ in range(2)]; b_ofall2 = P.bufs(2, "ofall")
        osall = AR.bf16(32, 128); b_osall = P.buf("osall")
        rz = AR.f32(2, 2); b_rz = P.bufs(2, "rz")
        ssH = AR.f32(32); rsH = AR.f32(32); b_ssH = P.buf("ssH")
        rb_sb = AR.f32(16); ohs = AR.f32(512); fsb = AR.f32(512); b_rb = P.buf("rb"); b_oh = P.buf("oh"); b_fsb = P.buf("fsb"); b_FS = P.buf("FS")
        P.dma("sp", cs(), lambda e: e.dma_start(out=rb_sb[0:32, :], in_=relb), writes=[b_rb])
        P.dma("sp", cs(), lambda e: e.dma_start(out=ohs[0:32, :], in_=c_oh), writes=[b_oh])
        P.op("pe", lambda e: e.matmul(bank(0)[0:16, :], lhsT=rb_sb[0:32, :], rhs=ohs[0:32, :], start=True, stop=True), reads=[b_rb, b_oh], writes=[bps[0]])
        P.op("dve", lambda e: e.tensor_copy(out=fsb[0:16, :], in_=bank(0)[0:16, :]), reads=[bps[0]], writes=[b_fsb])
        P.dma("sp", cs(), lambda e: e.dma_start(out=FSCR, in_=fsb[0:16, :]), reads=[b_fsb], writes=[b_FS])
        for col in range(16):
            srcH = bass.AP(FSCR.tensor, FSCR.offset + col * 512, [[1, 128], [128, 3], [1, 128]])
            P.dma("sp", cs(), lambda e, col=col, srcH=srcH: e.dma_start(out=HR[:, col * 3:(col + 1) * 3, :], in_=srcH), reads=[b_FS], writes=[b_HR])
        hra = HR
        rev = bass.AP(hra.tensor, hra.offset + 127, [list(hra.ap[0]), [128, 48], [-1, 128]])
        P.dma("sp", cs(), lambda e: e.dma_start(out=cm, in_=bcast_row(relb, 16, 15 * 16)), writes=[b_cc])
        P.dma("sp", cs(), lambda e: e.dma_start(out=cp, in_=bcast_row(relb, 16, 31 * 16)), writes=[b_cc])
        P.op("dve", lambda e: e.tensor_copy(out=ET, in_=rev), reads=[b_HR], writes=[b_ET])
        for col in range(16):
            P.op("dve", lambda e, col=col: e.tensor_scalar(out=ET[:, col * 3:(col + 1) * 3, :], in0=ET[:, col * 3:(col + 1) * 3, :], scalar1=cm[:, col:col + 1], scalar2=None, op0=ALU.subtract),
                 reads=[b_ET, b_cc], writes=[b_ET])
        P.op("act", lambda e: e.activation(out=ET, in_=ET, func=AF.Exp), reads=[b_ET], writes=[b_ET])
        P.op("dve", lambda e: e.tensor_tensor(out=cdm, in0=cp, in1=cm, op=ALU.subtract), reads=[b_cc], writes=[b_ecd])
        P.op("act", lambda e: e.activation(out=ecd, in_=cdm, func=AF.Exp), reads=[b_ecd], writes=[b_ecd])
        P.dma("sp", cs(), lambda e: e.dma_start(out=agb, in_=bcast_row(att_g, 1024)), writes=[b_agb])
        for i, la in enumerate((lq1, lk1, lq2, lk2)):
            P.dma("sp", cs(), lambda e, i=i, la=la: e.dma_start(out=lam4[:, i, :], in_=bcast_row(la, 64)), writes=[b_lam4])
        P.op("dve", lambda e: e.tensor_tensor(out=lprod[:, 0, :], in0=lam4[:, 0, :], in1=lam4[:, 1, :], op=ALU.mult), reads=[b_lam4], writes=[b_lam])
        P.op("dve", lambda e: e.tensor_tensor(out=lprod[:, 1, :], in0=lam4[:, 2, :], in1=lam4[:, 3, :], op=ALU.mult), reads=[b_lam4, b_lam], writes=[b_lam])
        P.op("dve", lambda e: e.tensor_reduce(out=lsum, in_=lprod, axis=AX.X, op=ALU.add), reads=[b_lam], writes=[b_lam])
        P.op("act", lambda e: e.activation(out=lsum, in_=lsum, func=AF.Exp), reads=[b_lam], writes=[b_lam])
        P.op("dve", lambda e: e.tensor_tensor(out=nlam, in0=lsum[:, 1:2], in1=lsum[:, 0:1], op=ALU.subtract), reads=[b_lam], writes=[b_lam])
        P.op("dve", lambda e: e.tensor_single_scalar(out=nlam, in_=nlam, scalar=-LAMBDA_INIT, op=ALU.add), reads=[b_lam], writes=[b_lam])
        for hb in range(2):
            P.op("pool", lambda e, hb=hb: e.memset(vA[hb][:, :, 128:129], 1.0), writes=[b_vA[hb]])
        cnt = {"pT": 0, "etmp": 0, "sb": 0, "oS": 0}
        import os as _os
        NH = int(_os.environ.get('K_DBG_HEADS', '8'))

        def issue_loads(h):
            hb = h % 2
            P.dma("sp", f"aq{hb}", lambda e, h=h, hb=hb: e.dma_start(out=qT[hb], in_=QT[h]), reads=[b_QT[h]], writes=[b_qT[hb]])
            P.dma("sp", f"ak{hb}", lambda e, h=h, hb=hb: e.dma_start(out=kT[hb], in_=KT[h]), reads=[b_KT[h]], writes=[b_kT[hb]])
            vsrc = VV[:, h * 128:(h + 1) * 128].rearrange("(c p) e -> p c e", p=128)
            P.dma("sp", f"av{hb}", lambda e, hb=hb, vsrc=vsrc: e.dma_start(out=vA[hb][:, :, 0:128], in_=vsrc), reads=[b_VV], writes=[b_vA[hb]])

        def compute_vP(h):
            hb = h % 2
            for m in range(2):
                P.op("dve", lambda e, m=m, hb=hb, h=h: e.tensor_scalar(out=vP[hb][m], in0=vA[hb], scalar1=ecd[:, 2 * h + m:2 * h + m + 1], scalar2=None, op0=ALU.mult),
                     reads=[b_vA[hb], b_ecd], writes=[b_vP[hb][m]])

        pending_head_end = []
        if NH > 0:
            issue_loads(0)
            compute_vP(0)
        for h in range(NH):
            hb = h % 2
            if h + 1 < NH:
                issue_loads(h + 1)
            q_, k_, v_ = qT[hb], kT[hb], vA[hb]
            ofall, b_ofall = ofall2[hb], b_ofall2[hb]
            NQT = int(_os.environ.get('K_DBG_QT', '16'))

            def emit_qk(it, q_=q_, k_=k_, hb=hb):
                qt_, pr_ = divmod(it, 16)
                sb_ = it % 2
                for kk in range(2):
                    kb_ = 2 * pr_ + kk
                    for m in range(2):
                        P.op("pe", lambda e, sb_=sb_, m=m, kb_=kb_, qt_=qt_, kk=kk: e.matmul(
                            bank(2 * sb_ + m)[:, kk * 256:(kk + 1) * 256], lhsT=k_[m * 64:(m + 1) * 64, kb_ * 128:(kb_ + 1) * 128],
                            rhs=q_[m * 64:(m + 1) * 64, qt_ * 256:qt_ * 256 + 256], start=True, stop=True),
                            reads=[b_qT[hb], b_kT[hb]], writes=[bps[2 * sb_ + m]])

            emit_qk(0)
            if NQT * 16 > 1:
                emit_qk(1)
            for qt in range(NQT):
                if qt == 2 and h + 1 < NH:
                    compute_vP(h + 1)
                if qt == 1 and pending_head_end:
                    pending_head_end.pop(0)()
                for pr in range(16):
                    it = qt * 16 + pr
                    sb_ = it % 2
                    pk = cnt["pT"] % 3
                    cnt["pT"] += 1
                    p_t = pT[pk]
                    ty = {}
                    for kk in range(2):
                        for i in range(2):
                            off = (2 * pr + kk) - (2 * qt + i)
                            ty[(kk, i)] = "m" if off <= -2 else ("p" if off >= 2 else off)
                    src = ps[:, 2 * sb_:2 * sb_ + 2, :]
                    P.op("act", lambda e, src=src, p_t=p_t: e.activation(out=p_t, in_=src, func=AF.Exp, scale=0.125),
                         reads=[bps[2 * sb_], bps[2 * sb_ + 1]], writes=[b_pT[pk]])
                    for kk in range(2):
                        for i in range(2):
                            c0 = kk * 256 + i * 128
                            t_ = ty[(kk, i)]
                            if t_ not in ("m", "p"):
                                for m in range(2):
                                    eti = ET[:, (2 * h + m) * 3 + (t_ + 1), :]
                                    P.op("dve", lambda e, p_t=p_t, c0=c0, m=m, eti=eti: e.tensor_tensor(out=p_t[:, m, c0:c0 + 128], in0=p_t[:, m, c0:c0 + 128], in1=eti, op=ALU.mult),
                                         reads=[b_pT[pk], b_ET], writes=[b_pT[pk]])
                    if it + 2 < NQT * 16:
                        emit_qk(it + 2)
                    for kk in range(2):
                        kb = 2 * pr + kk
                        for i in range(2):
                            for m in range(2):
                                if ty[(kk, i)] == "p":
                                    vv, bvv = vP[hb][m], b_vP[hb][m]
                                else:
                                    vv, bvv = v_, b_vA[hb]
                                P.op("pe", lambda e, i=i, m=m, p_t=p_t, kb=kb, vv=vv, kk=kk: e.matmul(
                                    bank(4 + 2 * i + m)[:, 0:129], lhsT=p_t[:, m, kk * 256 + i * 128:kk * 256 + (i + 1) * 128], rhs=vv[:, kb, :], start=(kb == 0), stop=(kb == 31)),
                                    reads=[b_pT[pk], bvv], writes=[bps[4 + 2 * i + m]])
                P.op("dve", lambda e: e.tensor_copy(out=ocp, in_=ps[:, 4:8, 0:129]), reads=[bps[4], bps[5], bps[6], bps[7]], writes=[b_ocp])
                for i in range(2):
                    blk = 2 * qt + i
                    P.op("dve", lambda e, i=i: e.reciprocal(out=rz[:, i, :], in_=ocp[:, 2 * i:2 * i + 2, 128]), reads=[b_ocp], writes=[b_rz[i]])
                    P.op("dve", lambda e, i=i: e.tensor_tensor(out=rz[:, i, 1:2], in0=rz[:, i, 1:2], in1=nlam, op=ALU.mult), reads=[b_rz[i], b_lam], writes=[b_rz[i]])
                    P.op("dve", lambda e, i=i, blk=blk, ofall=ofall: e.tensor_scalar(out=ofall[:, blk, :], in0=ocp[:, 2 * i, 0:128], scalar1=rz[:, i, 0:1], scalar2=None, op0=ALU.mult),
                         reads=[b_ocp, b_rz[i]], writes=[b_ofall])
                    P.op("dve", lambda e, i=i, blk=blk, ofall=ofall: e.scalar_tensor_tensor(out=ofall[:, blk, :], in0=ocp[:, 2 * i + 1, 0:128], scalar=rz[:, i, 1:2], in1=ofall[:, blk, :], op0=ALU.mult, op1=ALU.add),
                         reads=[b_ocp, b_rz[i], b_ofall], writes=[b_ofall])
            def head_end(h=h, ofall=ofall, b_ofall=b_ofall, NQT=NQT):
                sqv = HR[:, 0:32, :]
                nblk = 2 * NQT
                P.op("dve", lambda e, sqv=sqv: e.tensor_tensor(out=sqv, in0=ofall, in1=ofall, op=ALU.mult), reads=[b_ofall], writes=[b_HR])
                P.op("dve", lambda e, sqv=sqv: e.tensor_reduce(out=ssH, in_=sqv, axis=AX.X, op=ALU.add), reads=[b_HR], writes=[b_ssH])
                P.op("dve", lambda e: e.tensor_scalar(out=rsH, in0=ssH, scalar1=1.0 / 128, scalar2=EPS, op0=ALU.mult, op1=ALU.add), reads=[b_ssH], writes=[b_ssH])
                P.op("act", lambda e: e.activation(out=rsH, in_=rsH, func=AF.Ln), reads=[b_ssH], writes=[b_ssH])
                P.op("act", lambda e: e.activation(out=rsH, in_=rsH, func=AF.Exp, scale=-0.5, bias=float(math.log(1.0 - LAMBDA_INIT))), reads=[b_ssH], writes=[b_ssH])
                P.op("dve", lambda e: e.tensor_tensor(out=ofall, in0=ofall, in1=rsH.unsqueeze(2).to_broadcast([128, 32, 128]), op=ALU.mult), reads=[b_ofall, b_ssH], writes=[b_ofall])
                ag_bc = bass.AP(agb.tensor, agb.offset + h * 128, [list(agb.ap[0]), [0, 32], [1, 128]])
                P.op("dve", lambda e, ag_bc=ag_bc: e.tensor_tensor(out=osall, in0=ofall, in1=ag_bc, op=ALU.mult), reads=[b_ofall, b_agb], writes=[b_osall])
                cdv = CONCAT[:, h * 128:(h + 1) * 128].rearrange("(c p) e -> p c e", p=128)
                P.dma("pool", "so0", lambda e, cdv=cdv: e.dma_start(out=cdv, in_=osall), reads=[b_osall], writes=[b_CC])

            pending_head_end.append(head_end)
        while pending_head_end:
            pending_head_end.pop(0)()
        P.barrier(barscr)

    if "C" in phases:
        AR.off = persist_off
        trif = AR.f32(128); trib = AR.f32(128); onesf = AR.f32(128); b_tri = P.buf("tri")
        P.dma("sp", cs(), lambda e: e.dma_start(out=trif, in_=c_trif), writes=[b_tri])
        P.dma("sp", cs(), lambda e: e.dma_start(out=trib, in_=c_trib), writes=[b_tri])
        P.dma("sp", cs(), lambda e: e.dma_start(out=onesf, in_=c_ones), writes=[b_tri])
        gt = AR.f32(32, 16); b_gt = P.buf("gt")
        lfn = AR.f32(32, 8); b_lfn = P.buf("lfn")
        nb = AR.f32(32, 8); nbl = AR.f32(32, 8); b_nb = P.buf("nb")
        alpha = AR.f32(32, 8); beta = AR.f32(32, 8); decay = AR.f32(32, 8); b_abd = P.buf("abd")
        cw = AR.f32(5, 16); cb = AR.f32(16); b_cw = P.buf("cw")
        mgb = AR.f32(1024); b_mgb = P.buf("mgb")
        P.dma("sp", cs(), lambda e: e.dma_start(out=gt, in_=GATES.rearrange("(c p) g -> p c g", p=128)), reads=[b_GA], writes=[b_gt])
        for j in range(5):
            srcw = conv_w[j:j + 1, :].rearrange("o (c p) -> p (o c)", p=128)
            P.dma("sp", cs(), lambda e, j=j, srcw=srcw: e.dma_start(out=cw[:, j, :], in_=srcw, allow_slow_non_contiguous=True), writes=[b_cw])
        load_gT(cb, conv_b, b_cw)
        P.dma("sp", cs(), lambda e: e.dma_start(out=mgb, in_=bcast_row(ml_g, 1024)), writes=[b_mgb])
        P.op("act", lambda e: e.activation(out=lfn, in_=gt[:, :, 8:16], func=AF.Exp, scale=-1.0), reads=[b_gt], writes=[b_lfn])
        P.op("act", lambda e: e.activation(out=lfn, in_=lfn, func=AF.Ln, bias=1.0), reads=[b_lfn], writes=[b_lfn])
        P.op("pe", lambda e: e.matmul(bank(0)[:, 0:128], lhsT=trif, rhs=lfn[:, :, 0:4], start=True, stop=True), reads=[b_tri, b_lfn], writes=[bps[0]])
        P.op("pe", lambda e: e.matmul(bank(0)[:, 128:256], lhsT=trib, rhs=lfn[:, :, 4:8], start=True, stop=True), reads=[b_tri, b_lfn], writes=[bps[0]])
        P.op("pe", lambda e: e.matmul(bank(1)[:, 0:256], lhsT=onesf, rhs=lfn, start=True, stop=True), reads=[b_tri, b_lfn], writes=[bps[1]])
        for d_ in range(2):
            P.op("dve", lambda e, d_=d_: e.tensor_copy(out=nb[:, :, d_ * 4:(d_ + 1) * 4], in_=bank(0)[:, d_ * 128:(d_ + 1) * 128].rearrange("p (c h) -> p c h", h=4)),
                 reads=[bps[0]], writes=[b_nb])
        P.op("dve", lambda e: e.tensor_copy(out=nbl, in_=bank(1)[:, 0:256].rearrange("p (c h) -> p c h", h=8)), reads=[bps[1]], writes=[b_nb])
        P.op("act", lambda e: e.activation(out=alpha, in_=nb, func=AF.Exp, scale=-1.0, bias=float(-math.log(16.0))), reads=[b_nb], writes=[b_abd])
        P.op("dve", lambda e: e.tensor_tensor(out=beta, in0=gt[:, :, 0:8], in1=nb, op=ALU.add), reads=[b_gt, b_nb], writes=[b_abd])
        P.op("act", lambda e: e.activation(out=beta, in_=beta, func=AF.Exp), reads=[b_abd], writes=[b_abd])
        P.op("act", lambda e: e.activation(out=decay, in_=nbl, func=AF.Exp, scale=-1.0), reads=[b_nb], writes=[b_abd])

        qTm = AR.bf16(2, S); kTm = AR.bf16(2, S); b_qTm = P.bufs(2, "qTm"); b_kTm = P.bufs(2, "kTm")
        raw = [AR.bf16(1028) for _ in range(2)]; b_raw = P.bufs(2, "raw")
        cacc = [AR.f32(1024) for _ in range(2)]; b_cacc = P.bufs(2, "cacc")
        kTok = AR.bf16(32, 256); b_kTok = P.buf("kTok")
        vt = AR.bf16(32, 257); b_vt = P.buf("vt")
        rawo = [AR.f32(32, 257) for _ in range(2)]; b_ro = [P.bufs(32, f"ro{i}") for i in range(2)]
        hm = rawo[0][:, :, 0:256]; hm1 = rawo[1][:, :, 0:256]
        b_hm = b_ro[0]
        dnb = AR.f32(4, 32); rrb = AR.f32(2, 32); b_dnb = P.buf("dnb")
        mo = AR.bf16(32, 256); b_mo = P.buf("mo")
        sq = AR.f32(8, 256); b_sq = P.buf("sq")
        ssm = AR.f32(32); rsm = AR.f32(32); b_ssm = P.buf("ssm")
        CT = [AR.f32(2, 257) for _ in range(2)]; b_CT = P.bufs(2, "CT")
        CTb = [AR.bf16(2, 257) for _ in range(2)]; b_CTb = P.bufs(2, "CTb")
        ctmp = [AR.f32(257) for _ in range(2)]; b_ctmp = P.bufs(2, "ctmp")
        vaug = [AR.bf16(257) for _ in range(4)]; b_vaug = P.bufs(4, "vaug")
        Sm = [AR.bf16(128) for _ in range(4)]; b_Sm = P.bufs(4, "Sm")
        dn = [AR.f32(4) for _ in range(4)]; b_dn = P.bufs(4, "dn")
        for r in raw:
            pass
        P.op("pool", lambda e: e.memset(raw[0][:, 0:1028], 0.0), writes=[b_raw[0]])
        P.op("pool", lambda e: e.memset(raw[1][:, 0:1028], 0.0), writes=[b_raw[1]])
        P.op("pool", lambda e: e.memset(vt[:, :, 256:257], 1.0), writes=[b_vt])
        cnt = {"raw": 0, "sb": 0, "va": 0, "ct": 0}
        for hd in range(4):
            for (src_d, b_src, dstT, b_dstT, qk) in ((MQ, b_MQ, qTm, b_qTm, 0), (MK, b_MK, kTm, b_kTm, 1)):
                for dc in range(2):
                    ch = hd * 2 + dc
                    cch = qk * 8 + ch
                    for pc in range(4):
                        t0 = pc * 1024
                        k = cnt["raw"] % 2
                        cnt["raw"] += 1
                        rw, ca = raw[k], cacc[k]
                        lo = max(t0 - 2, 0); hi = min(t0 + 1026, S)
                        dlo = lo - (t0 - 2)
                        if pc == 0 or pc == 3:
                            P.op("dve", lambda e, rw=rw: e.memset(rw[:, 0:1028], 0.0), writes=[b_raw[k]])
                        P.dma("sp", f"rw{k}", lambda e, rw=rw, src_d=src_d, ch=ch, lo=lo, hi=hi, dlo=dlo: e.dma_start(out=rw[:, dlo:dlo + (hi - lo)], in_=src_d[ch][:, lo:hi]),
                              reads=[b_src[ch]], writes=[b_raw[k]])
                        P.op("dve", lambda e, rw=rw, ca=ca, cch=cch: e.tensor_scalar(out=ca, in0=rw[:, 0:1024], scalar1=cw[:, 0, cch:cch + 1], scalar2=None, op0=ALU.mult),
                             reads=[b_raw[k], b_cw], writes=[b_cacc[k]])
                        for j in range(1, 5):
                            P.op("dve", lambda e, rw=rw, ca=ca, cch=cch, j=j: e.scalar_tensor_tensor(out=ca, in0=rw[:, j:j + 1024], scalar=cw[:, j, cch:cch + 1], in1=ca, op0=ALU.mult, op1=ALU.add),
                                 reads=[b_raw[k], b_cw, b_cacc[k]], writes=[b_cacc[k]])
                        P.op("act", lambda e, ca=ca, cch=cch, dstT=dstT, dc=dc, t0=t0: e.activation(out=dstT[:, dc, t0:t0 + 1024], in_=ca, func=AF.Silu, bias=cb[:, cch:cch + 1]),
                             reads=[b_cacc[k], b_cw], writes=[b_dstT[dc]])
            for c4 in range(8):
                bk = cnt["sb"] % 4
                cnt["sb"] += 1
                pb = bankbf(bk)
                for cc_ in range(4):
                    c = c4 * 4 + cc_
                    for dc in range(2):
                        P.op("pe", lambda e, pb=pb, cc_=cc_, dc=dc, c=c: e.transpose(out=pb[:, (cc_ * 2 + dc) * 128:(cc_ * 2 + dc + 1) * 128], in_=kTm[:, dc, c * 128:(c + 1) * 128], identity=ident),
                             reads=[b_kTm[dc], b_ident], writes=[bps[bk]])
                copy_op(alt_eng(), kTok[:, c4 * 4:(c4 + 1) * 4, :], pb[:, 0:1024].rearrange("p (a b) -> p a b", a=4), [bps[bk]], [b_kTok])
            vsrc = MV[:, hd * 256:(hd + 1) * 256].rearrange("(c p) e -> p c e", p=128)
            P.dma("sp", "mv", lambda e, vsrc=vsrc: e.dma_start(out=vt[:, :, 0:256], in_=vsrc), reads=[b_MV], writes=[b_vt])
            osrc = MO[:, hd * 256:(hd + 1) * 256].rearrange("(c p) e -> p c e", p=128)
            P.dma("sp", "mo", lambda e, osrc=osrc: e.dma_start(out=mo, in_=osrc), reads=[b_MO], writes=[b_mo])
            for step in range(32):
                for dr in range(2):
                    c = step if dr == 0 else 31 - step
                    gi = dr * 4 + hd
                    va_k = cnt["va"] % 4
                    cnt["va"] += 1
                    va, smt, dnt = vaug[va_k], Sm[va_k], dn[va_k]
                    P.op("act", lambda e, va=va, c=c, gi=gi: e.activation(out=va, in_=vt[:, c, :], func=AF.Copy, scale=beta[:, c, gi:gi + 1]),
                         reads=[b_vt, b_abd], writes=[b_vaug[va_k]])
                    sbk = cnt["sb"] % 4
                    cnt["sb"] += 1
                    for dc in range(2):
                        P.op("pe", lambda e, sbk=sbk, dc=dc, c=c: e.matmul(bank(sbk)[:, 0:128], lhsT=kTm[:, dc, c * 128:(c + 1) * 128], rhs=qTm[:, dc, c * 128:(c + 1) * 128], start=(dc == 0), stop=(dc == 1)),
                             reads=[b_kTm[0], b_kTm[1], b_qTm[0], b_qTm[1]], writes=[bps[sbk]])
                    msk = trif if dr == 0 else trib
                    P.op("dve", lambda e, smt=smt, sbk=sbk, msk=msk: e.tensor_tensor(out=smt, in0=bank(sbk)[:, 0:128], in1=msk, op=ALU.mult),
                         reads=[bps[sbk], b_tri], writes=[b_Sm[va_k]])
                    obk = 4 + (cnt["ct"] % 2)
                    first = (step == 0)
                    P.op("pe", lambda e, obk=obk, smt=smt, va=va, first=first: e.matmul(bank(obk)[:, 0:257], lhsT=smt, rhs=va, start=True, stop=first),
                         reads=[b_Sm[va_k], b_vaug[va_k]], writes=[bps[obk]])
                    if not first:
                        for dc in range(2):
                            P.op("pe", lambda e, obk=obk, dc=dc, c=c, dr=dr: e.matmul(bank(obk)[:, 0:257], lhsT=qTm[:, dc, c * 128:(c + 1) * 128], rhs=CTb[dr][:, dc, :], start=False, stop=(dc == 1)),
                                 reads=[b_qTm[dc], b_CTb[dr]], writes=[bps[obk]])
                    P.op("dve", lambda e, c=c, obk=obk, dr=dr: e.tensor_copy(out=rawo[dr][:, c, :], in_=bank(obk)[:, 0:257]),
                         reads=[bps[obk]], writes=[b_ro[dr][c]])
                    cnt["ct"] += 1
                    if step < 31:
                        for dc in range(2):
                            ubk = 6 + dc
                            P.op("pe", lambda e, ubk=ubk, dc=dc, c=c, va=va: e.matmul(bank(ubk)[:, 0:257], lhsT=kTok[:, c, dc * 128:(dc + 1) * 128], rhs=va, start=True, stop=True),
                                 reads=[b_kTok, b_vaug[va_k]], writes=[bps[ubk]])
                            cprev = c - 1 if dr == 0 else c + 1
                            if first:
                                P.op("dve", lambda e, ubk=ubk, dc=dc, dr=dr: e.tensor_copy(out=CT[dr][:, dc, :], in_=bank(ubk)[:, 0:257]),
                                     reads=[bps[ubk]], writes=[b_CT[dr]])
                            else:
                                P.op("dve", lambda e, ubk=ubk, dc=dc, dr=dr, cprev=cprev, gi=gi: e.scalar_tensor_tensor(out=CT[dr][:, dc, :], in0=CT[dr][:, dc, :], scalar=decay[:, cprev, gi:gi + 1], in1=bank(ubk)[:, 0:257], op0=ALU.mult, op1=ALU.add),
                                     reads=[bps[ubk], b_CT[dr], b_abd], writes=[b_CT[dr]])
                            P.op("act", lambda e, dc=dc, dr=dr, c=c, gi=gi: e.activation(out=CTb[dr][:, dc, :], in_=CT[dr][:, dc, :], func=AF.Copy, scale=decay[:, c, gi:gi + 1]),
                                 reads=[b_CT[dr], b_abd], writes=[b_CTb[dr]])
            allro = b_ro[0] + b_ro[1]
            for dr in range(2):
                gi = dr * 4 + hd
                al = alpha[:, :, gi]
                d0, d1, d2 = dnb[:, 0, :], dnb[:, 1, :], dnb[:, 2, :]
                P.op("dve", lambda e, dr=dr, al=al, d0=d0: e.tensor_tensor(out=d0, in0=rawo[dr][:, :, 256], in1=al, op=ALU.mult), reads=b_ro[dr] + [b_abd, b_dnb], writes=[b_dnb])
                P.op("dve", lambda e, d0=d0, d1=d1: e.tensor_single_scalar(out=d1, in_=d0, scalar=-1.0, op=ALU.mult), reads=[b_dnb], writes=[b_dnb])
                P.op("dve", lambda e, d0=d0, d1=d1, d2=d2: e.scalar_tensor_tensor(out=d2, in0=d0, scalar=1.0, in1=d1, op0=ALU.max, op1=ALU.max), reads=[b_dnb], writes=[b_dnb])
                P.op("dve", lambda e, d2=d2, dr=dr: e.reciprocal(out=rrb[:, dr, :], in_=d2), reads=[b_dnb], writes=[b_dnb])
                P.op("dve", lambda e, dr=dr, al=al: e.tensor_tensor(out=rrb[:, dr, :], in0=rrb[:, dr, :], in1=al, op=ALU.mult), reads=[b_dnb, b_abd], writes=[b_dnb])
            P.op("dve", lambda e: e.tensor_tensor(out=hm, in0=hm, in1=rrb[:, 0, :].unsqueeze(2).to_broadcast([128, 32, 256]), op=ALU.mult), reads=b_ro[0] + [b_dnb], writes=b_ro[0])
            P.op("dve", lambda e: e.tensor_tensor(out=hm1, in0=hm1, in1=rrb[:, 1, :].unsqueeze(2).to_broadcast([128, 32, 256]), op=ALU.mult), reads=b_ro[1] + [b_dnb], writes=b_ro[1])
            P.op("dve", lambda e: e.tensor_tensor(out=hm, in0=hm, in1=hm1, op=ALU.add), reads=allro, writes=b_ro[0])
            for pc in range(4):
                P.op("dve", lambda e, pc=pc: e.tensor_tensor(out=sq, in0=hm[:, pc * 8:(pc + 1) * 8, :], in1=hm[:, pc * 8:(pc + 1) * 8, :], op=ALU.mult),
                     reads=b_hm[pc * 8:(pc + 1) * 8], writes=[b_sq])
                P.op("dve", lambda e, pc=pc: e.tensor_reduce(out=ssm[:, pc * 8:(pc + 1) * 8], in_=sq, axis=AX.X, op=ALU.add), reads=[b_sq], writes=[b_ssm])
            P.op("dve", lambda e: e.tensor_scalar(out=rsm, in0=ssm, scalar1=1.0 / 256, scalar2=EPS, op0=ALU.mult, op1=ALU.add), reads=[b_ssm], writes=[b_ssm])
            P.op("act", lambda e: e.activation(out=rsm, in_=rsm, func=AF.Ln), reads=[b_ssm], writes=[b_ssm])
            P.op("act", lambda e: e.activation(out=rsm, in_=rsm, func=AF.Exp, scale=-0.5), reads=[b_ssm], writes=[b_ssm])
            P.op("act", lambda e: e.activation(out=mo, in_=mo, func=AF.Sigmoid), reads=[b_mo], writes=[b_mo])
            P.op("dve", lambda e: e.tensor_tensor(out=hm, in0=hm, in1=rsm.unsqueeze(2).to_broadcast([128, 32, 256]), op=ALU.mult), reads=b_hm + [b_ssm], writes=b_hm)
            mg_bc = bass.AP(mgb.tensor, mgb.offset + hd * 256, [list(mgb.ap[0]), [0, 32], [1, 256]])
            P.op("dve", lambda e, mg_bc=mg_bc: e.tensor_tensor(out=hm, in0=hm, in1=mg_bc, op=ALU.mult), reads=b_hm + [b_mgb], writes=b_hm)
            P.op("dve", lambda e: e.tensor_tensor(out=mo, in0=hm, in1=mo, op=ALU.mult), reads=b_hm + [b_mo], writes=[b_mo])
            odv = CONCAT[:, 1024 + hd * 256:1024 + (hd + 1) * 256].rearrange("(c p) e -> p c e", p=128)
            P.dma("pool", "sm", lambda e, odv=odv: e.dma_start(out=odv, in_=mo), reads=[b_mo], writes=[b_CC])
        P.barrier(barscr)

    if "D" in phases:
        AR.off = persist_off
        A = ffn_phase_alloc(2)
        load_gT(A["gT"][0], gv["ffn2_pre_g"], A["b_gT"][0])
        P.dma("sp", cs(), lambda e: e.dma_start(out=A["gb"][0], in_=bcast_row(gv["mix_post_g"], D)), writes=[A["b_gb"][0]])
        P.dma("sp", cs(), lambda e: e.dma_start(out=A["gb"][1], in_=bcast_row(gv["ffn2_post_g"], D)), writes=[A["b_gb"][1]])
        xr0 = A["xr"][0]
        ccv = bass.AP(xr0.tensor, xr0.offset, [list(xr0.ap[0]), [1, 4096]]).bitcast(BF16).rearrange("p (b d) -> p b d", b=4)
        b_ccb = [A["b_xr"][0], A["b_xr"][0], A["b_xr"][1], A["b_xr"][1]]
        for t in range(NT):
            csrc = CONCAT[t * 512:(t + 1) * 512, :].rearrange("(b p) d -> p b d", p=128)
            P.dma("sp", "xl", lambda e, csrc=csrc: e.dma_start(out=ccv, in_=csrc), reads=[b_CC], writes=[A["b_xr"][0], A["b_xr"][1]])
            norm_transpose(A, None, None, src=ccv, b_src=b_ccb, scale=False)
            hT = A["hT"]
            for g in range(8):
                s = A["cnt"]["wgu"] % 2
                A["cnt"]["wgu"] += 1
                w_t = A["wgs"][s]
                P.dma("sp", f"wg{s}", lambda e, g=g, w_t=w_t: e.dma_start(out=w_t, in_=swout[g]), reads=[b_wout[g]], writes=[A["b_wgs"][s]])
                for b in range(4):
                    bk = next_bank()
                    for c in range(16):
                        P.op("pe", lambda e, bk=bk, c=c, b=b, w_t=w_t: e.matmul(bank(bk)[:, 0:256], lhsT=hT[:, c, b * 128:(b + 1) * 128], rhs=w_t[:, c, :], start=(c == 0), stop=(c == 15)),
                             reads=[A["b_wgs"][s], A["b_hT"]], writes=[bps[bk]])
                    copy_op(alt_eng(), A["xy"][:, b, g * 256:(g + 1) * 256], bank(bk)[:, 0:256], [bps[bk]], [A["b_xy"][b]])
            for hf in range(2):
                post_residual(A, t, X1, b_X1, A["gb"][0], A["b_gb"][0], 1.0, X2, b_X2, False, hf)
                sumsq_blocks(A, hf)
                stats_rstd(A, D, 1.0, hf)
                norm_transpose(A, A["gT"][0], A["b_gT"][0], blocks=(2 * hf, 2 * hf + 1))
            ffn(A, sw2g, sw2u, sw2d, b_w2g, b_w2u, b_w2d)
            for hf in range(2):
                post_residual(A, t, X2, b_X2, A["gb"][1], A["b_gb"][1], 0.5, y, None, True, hf)

    if not final_ops:
        final_ops.append(P.barrier(barscr))
        for s, l in P.streams.items():
            final_ops.append(l[-1])
    P.emit_all(nc, final_ops)
    es.close()
    return nc, P


_CACHE = {}
W_NAMES = ["rel_bias", "ffn1_pre_g", "ffn1_post_g", "ffn1_w_gate", "ffn1_w_up", "ffn1_w_down", "mix_pre_g", "mix_post_g",
           "w_in", "gate_bias", "conv_w", "conv_b", "lambda_q1", "lambda_k1", "lambda_q2", "lambda_k2",
           "att_head_g", "mlstm_head_g", "w_out", "ffn2_pre_g", "ffn2_post_g", "ffn2_w_gate", "ffn2_w_up", "ffn2_w_down"]


def shared_inputs(inputs):
    sh = {}
    for n in W_NAMES:
        a = np.ascontiguousarray(np.asarray(inputs[n], dtype=np.float32))
        if n != "rel_bias":
            a = a[0]
            if a.ndim == 1:
                a = a[None, :]
        sh[n] = np.ascontiguousarray(a)
    sh.update(host_consts())
    return sh


def kernel(**inputs):
    if "nc" not in _CACHE:
        _CACHE["nc"], _ = build_program()
    nc = _CACHE["nc"]
    xp = np.asarray(inputs["x_prompt"], dtype=np.float32)
    xs = np.asarray(inputs["x_sample"], dtype=np.float32)
    seqs = [xp[0], xp[1], xs[0], xs[1], xs[2], xs[3], xp[0], xp[1]]
    sh = shared_inputs(inputs)
    in_maps = []
    for c in range(8):
        m = dict(sh)
        m["x"] = np.ascontiguousarray(seqs[c])
        in_maps.append(m)
    res = run_bass_kernel_spmd(nc, in_maps, core_ids=list(range(8)))
    outs = [np.asarray(res.results[c]["y"], dtype=np.float32) for c in range(6)]
    y_prompt = np.stack(outs[0:2], axis=0)
    y_sample = np.stack(outs[2:6], axis=0)
    return (y_prompt, y_sample)
```

```python
import math
from contextlib import ExitStack

import numpy as np
import ml_dtypes

import concourse.bass as bass
import concourse.mybir as mybir
from concourse.bass_utils import run_bass_kernel_spmd

F32 = mybir.dt.float32
BF16 = mybir.dt.bfloat16
AF = mybir.ActivationFunctionType
ALU = mybir.AluOpType
AX = mybir.AxisListType

S = 4096
D = 2048
FF = 5632
NT = 8
EPS = 1e-6
LAMBDA_INIT = 0.8 - 0.6 * math.exp(-0.3 * 0)


class Buf:
    __slots__ = ("name", "w", "r")

    def __init__(self, name):
        self.name = name
        self.w = None
        self.r = {}


class Op:
    __slots__ = ("idx", "eng", "emit", "deps", "is_dma", "stream", "count", "signal")

    def __init__(self, idx, eng, emit, is_dma, stream):
        self.idx = idx
        self.eng = eng
        self.emit = emit
        self.deps = []
        self.is_dma = is_dma
        self.stream = stream
        self.count = 0
        self.signal = False


class Prog:
    ENGS = ("pe", "act", "dve", "pool", "sp")
    SAME_ENGINE_SYNC = {"pe": False, "act": True, "dve": True, "pool": True, "sp": False}

    def __init__(self):
        self.ops = []
        self.by_eng = {e: [] for e in self.ENGS}
        self.streams = {}
        self.nbuf = 0
        self.barrier_op = None
        self.barrier_seen = set()
        self.tokens = {}

    def buf(self, name=None):
        self.nbuf += 1
        return Buf(name or f"b{self.nbuf}")

    def bufs(self, n, name="b"):
        return [self.buf(f"{name}{i}") for i in range(n)]

    @staticmethod
    def _key(op):
        return ("d", op.stream) if op.is_dma else ("e", op.eng)

    def _add(self, eng, emit, reads, writes, is_dma, stream):
        op = Op(len(self.ops), eng, emit, is_dma, stream)
        deps = {}
        for b in reads:
            if b.w is not None:
                deps[b.w.idx] = b.w
        for b in writes:
            if b.w is not None:
                deps[b.w.idx] = b.w
            for r in b.r.values():
                deps[r.idx] = r
        if self.barrier_op is not None and eng not in self.barrier_seen:
            deps[self.barrier_op.idx] = self.barrier_op
            self.barrier_seen.add(eng)
        op.deps = list(deps.values())
        k = self._key(op)
        for b in reads:
            b.r[k] = op
        for b in writes:
            b.w = op
            b.r = {}
        self.ops.append(op)
        self.by_eng[eng].append(op)
        if is_dma:
            self.streams.setdefault(stream, []).append(op)
        return op

    def op(self, eng, emit, reads=(), writes=()):
        return self._add(eng, emit, reads, writes, False, None)

    def dma(self, eng, stream, emit, reads=(), writes=()):
        tok = self.tokens.get(stream)
        if tok is None:
            tok = self.tokens[stream] = Buf('tok_' + stream)
        return self._add(eng, emit, reads, list(writes) + [tok], True, stream)

    def barrier(self, scratch_ap):
        deps = []
        for e in self.ENGS:
            for o in reversed(self.by_eng[e]):
                if not o.is_dma:
                    deps.append(o)
                    break
        for s, l in self.streams.items():
            deps.append(l[-1])
        op = Op(len(self.ops), "pool", lambda e: e.memset(scratch_ap, 0.0), False, None)
        op.deps = deps
        self.ops.append(op)
        self.by_eng["pool"].append(op)
        self.barrier_op = op
        self.barrier_seen = {"pool"}
        return op

    def emit_all(self, nc, final_wait_ops=()):
        def need(op, d):
            if d.is_dma:
                return True
            if d.eng == op.eng and not op.is_dma and not self.SAME_ENGINE_SYNC[d.eng]:
                return False
            return True

        for op in self.ops:
            for d in op.deps:
                if not d.is_dma and need(op, d):
                    d.signal = True
        for d in final_wait_ops:
            if not d.is_dma:
                d.signal = True
        es = ExitStack()
        sems = {e: es.enter_context(nc.semaphore(f"s_{e}")) for e in self.ENGS}
        ssem = {s: es.enter_context(nc.semaphore(f"d_{s}")) for s in self.streams}
        for e in self.ENGS:
            c = 0
            for op in self.by_eng[e]:
                if op.is_dma:
                    continue
                if op.signal:
                    c += 1
                op.count = c
        for s, lst in self.streams.items():
            for i, op in enumerate(lst):
                op.count = 16 * (i + 1)
        self.stats = {e: len(self.by_eng[e]) for e in self.ENGS}

        def waits_for(op, seen):
            best = {}
            for d in op.deps:
                if not need(op, d):
                    continue
                key = self._key(d)
                sem = ssem[d.stream] if d.is_dma else sems[d.eng]
                if seen.get(key, 0) >= d.count:
                    continue
                if key not in best or best[key][1] < d.count:
                    best[key] = (sem, d.count)
            for key, (sem, val) in best.items():
                seen[key] = val
            return list(best.values())

        block = es.enter_context(nc.Block())

        def run_engine(engname):
            def f(eng):
                seen = {}
                for op in self.by_eng[engname]:
                    for sem, val in waits_for(op, seen):
                        eng.wait_ge(sem, val)
                    ins = op.emit(eng)
                    if op.is_dma:
                        ins.then_inc(ssem[op.stream], 16)
                    elif op.signal:
                        ins.then_inc(sems[engname], 1)
                if engname == "sp":
                    for d in final_wait_ops:
                        eng.wait_ge(ssem[d.stream] if d.is_dma else sems[d.eng], d.count)
            return f

        block.tensor(run_engine("pe"))
        block.scalar(run_engine("act"))
        block.vector(run_engine("dve"))
        block.gpsimd(run_engine("pool"))
        block.sync(run_engine("sp"))
        es.close()


class Arena:
    def __init__(self, ap):
        self.ap = ap
        self.off = 0
        self.W = ap.shape[1]

    def _take(self, nbytes):
        w = ((nbytes + 31) // 32) * 8
        s = self.off
        self.off += w
        assert self.off <= self.W, f"SBUF arena overflow {self.off * 4} > {self.W * 4}"
        return s

    @staticmethod
    def _shape(v, shape):
        if len(shape) == 1:
            return v
        if len(shape) == 2:
            return v.rearrange("p (a b) -> p a b", a=shape[0])
        if len(shape) == 3:
            return v.rearrange("p (a b c) -> p a b c", a=shape[0], b=shape[1])
        raise ValueError

    def f32(self, *shape):
        n = int(np.prod(shape))
        s = self._take(4 * n)
        return self._shape(self.ap[:, s:s + n], shape)

    def bf16(self, *shape):
        n = int(np.prod(shape))
        s = self._take(2 * n)
        return self._shape(self.ap[:, s:s + (n + 1) // 2].bitcast(BF16)[:, :n], shape)


def rel_bucket_np(rel):
    nb = 16
    max_exact = 8
    ret = np.where(rel > 0, nb, 0)
    n = np.abs(rel)
    nf = np.maximum(n, 1).astype(np.float32)
    large = max_exact + (np.log(nf / np.float32(max_exact)) / np.float32(math.log(128 / max_exact))
                         * np.float32(nb - max_exact)).astype(np.int32)
    large = np.minimum(large, nb - 1)
    return ret + np.where(n < max_exact, n, large)


def host_consts():
    c = {}
    c["ident"] = np.eye(128, dtype=np.float32).astype(ml_dtypes.bfloat16)
    s_, t_ = np.meshgrid(np.arange(128), np.arange(128), indexing="ij")
    c["trif"] = (s_ <= t_).astype(np.float32)
    c["trib"] = (s_ >= t_).astype(np.float32)
    c["ones"] = np.ones((128, 128), np.float32)
    try:
        import jax
        import jax.numpy as jnp
        with jax.default_device(jax.devices("cpu")[0]):
            rel = jnp.arange(-255, 256)
            nb = 16
            max_exact = 8
            ret = jnp.where(rel > 0, nb, 0)
            n = jnp.abs(rel)
            nf = jnp.maximum(n, 1).astype(jnp.float32)
            large = max_exact + (jnp.log(nf / max_exact) / math.log(128 / max_exact) * (nb - max_exact)).astype(jnp.int32)
            large = jnp.minimum(large, nb - 1)
            bk = np.asarray(ret + jnp.where(n < max_exact, n, large))
    except Exception:
        bk = rel_bucket_np(np.arange(-255, 256))
    oh = np.zeros((32, 512), np.float32)
    oh[bk, np.arange(511)] = 1.0
    c["oh"] = oh
    return c


def build_program(dbg=False, phases="PABCD"):
    nc = bass.Bass("TRN2", target_bir_lowering=False)
    P = Prog()
    es = ExitStack()

    def din(name, shape, dt=F32):
        return nc.dram_tensor(name, list(shape), dt, kind="ExternalInput").ap()

    def scr(name, shape, dt):
        return nc.dram_tensor(name, list(shape), dt, kind=("ExternalOutput" if dbg else "Internal")).ap()

    x = din("x", [S, D])
    relb = din("rel_bias", [32, 16])
    g_names = ["ffn1_pre_g", "ffn1_post_g", "mix_pre_g", "mix_post_g", "ffn2_pre_g", "ffn2_post_g"]
    gv = {n: din(n, [1, D]) for n in g_names}
    w1g = din("ffn1_w_gate", [D, FF]); w1u = din("ffn1_w_up", [D, FF]); w1d = din("ffn1_w_down", [FF, D])
    w2g = din("ffn2_w_gate", [D, FF]); w2u = din("ffn2_w_up", [D, FF]); w2d = din("ffn2_w_down", [FF, D])
    w_in = din("w_in", [D, 7184]); w_out = din("w_out", [D, D])
    gate_bias = din("gate_bias", [1, 16]); conv_w = din("conv_w", [5, 2048]); conv_b = din("conv_b", [1, 2048])
    lq1 = din("lambda_q1", [1, 64]); lk1 = din("lambda_k1", [1, 64]); lq2 = din("lambda_q2", [1, 64]); lk2 = din("lambda_k2", [1, 64])
    att_g = din("att_head_g", [1, 1024]); ml_g = din("mlstm_head_g", [1, 1024])
    c_ident = din("ident", [128, 128], BF16); c_trif = din("trif", [128, 128]); c_trib = din("trib", [128, 128])
    c_ones = din("ones", [128, 128]); c_oh = din("oh", [32, 512])
    y = nc.dram_tensor("y", [S, D], F32, kind="ExternalOutput").ap()

    sw1g = scr("sw1g", [22, 128, 16, 256], BF16); sw1u = scr("sw1u", [22, 128, 16, 256], BF16)
    sw2g = scr("sw2g", [22, 128, 16, 256], BF16); sw2u = scr("sw2u", [22, 128, 16, 256], BF16)
    sw1d = scr("sw1d", [4, 128, 44, 512], BF16); sw2d = scr("sw2d", [4, 128, 44, 512], BF16)
    swin = scr("swin", [28, 128, 16, 256], BF16); swgt = scr("swgt", [128, 16, 16], BF16)
    swout = scr("swout", [8, 128, 16, 256], BF16)
    X1 = scr("X1", [S, D], F32); X2 = scr("X2", [S, D], F32)
    QT = scr("QT", [8, 128, S], BF16); KT = scr("KT", [8, 128, S], BF16)
    MQ = scr("MQ", [8, 128, S], BF16); MK = scr("MK", [8, 128, S], BF16)
    VV = scr("VV", [S, 1024], BF16); MV = scr("MV", [S, 1024], BF16); MO = scr("MO", [S, 1024], BF16)
    GATES = scr("GATES", [S, 16], F32); CONCAT = scr("CONCAT", [S, 2048], BF16)
    FSCR = scr("FSCR", [16, 512], F32)

    arena_t = es.enter_context(nc.sbuf_tensor("arena", [128, 53000], F32))
    ps = es.enter_context(nc.psum_tensor("ps", [128, 8, 512], F32))
    bps = P.bufs(8, "ps")

    def bank(b):
        return ps[:, b, :]

    def bankbf(b):
        return ps[:, b, :].bitcast(BF16)

    AR = Arena(arena_t[:])
    rrs = {"c": 0, "wc": 0}

    def cs():
        rrs["c"] += 1
        return f"c{rrs['c'] % 4}"

    def wcs():
        rrs["wc"] += 1
        return f"wc{rrs['wc'] % 8}"

    ident = AR.bf16(128); b_ident = P.buf("ident")
    barscr = AR.f32(8)
    P.dma("sp", cs(), lambda e: e.dma_start(out=ident, in_=c_ident), writes=[b_ident])
    persist_off = AR.off

    def bcast_row(ap_row, n, off=0):
        return bass.AP(ap_row.tensor, ap_row.offset + off, [[0, 128], [1, n]])

    def load_gT(dst, g_ap, b):
        src = g_ap.rearrange("o (c p) -> p (o c)", p=128)
        P.dma("sp", cs(), lambda e: e.dma_start(out=dst, in_=src, allow_slow_non_contiguous=True), writes=[b])

    final_ops = []
    rr = {"bank": 0, "eng": 0}

    def next_bank():
        b = rr["bank"]
        rr["bank"] = (b + 1) % 4
        return b

    def alt_eng():
        rr["eng"] ^= 1
        return "act" if rr["eng"] else "dve"

    def copy_op(eng, out, in_, reads, writes):
        if eng == "act":
            P.op("act", lambda e: e.activation(out=out, in_=in_, func=AF.Copy), reads=reads, writes=writes)
        else:
            P.op(eng, lambda e: e.tensor_copy(out=out, in_=in_), reads=reads, writes=writes)

    PRE = []

    pre_gate = []

    def cast_dma(out, in_, wbuf):
        PRE.append(lambda: P.dma("pool", wcs(), lambda e: e.dma_start(out=out, in_=in_), reads=list(pre_gate), writes=[wbuf]))

    def cast_g256(w_ap, col0, ng, dst, name):
        wv = w_ap[:, col0:col0 + ng * 256].rearrange("(c p) (g j) -> g p c j", p=128, j=256)
        bs = P.bufs(ng, name)
        for g in range(ng):
            cast_dma(dst[g], wv[g], bs[g])
        return bs

    def cast_wd(w_ap, dst, name):
        wv = w_ap.rearrange("(f p) (n j) -> n p f j", p=128, j=512)
        bs = [P.bufs(4, f"{name}{n}_") for n in range(4)]
        for n in range(4):
            for fp in range(4):
                cast_dma(dst[n][:, fp * 11:(fp + 1) * 11, :], wv[n][:, fp * 11:(fp + 1) * 11, :], bs[n][fp])
        return bs

    wv1g = w1g.rearrange("(c p) (g j) -> g p c j", p=128, j=256)
    wv1u = w1u.rearrange("(c p) (g j) -> g p c j", p=128, j=256)
    b_w1g = P.bufs(22, "w1g"); b_w1u = P.bufs(22, "w1u")
    for g in range(22):
        cast_dma(sw1g[g], wv1g[g], b_w1g[g])
        cast_dma(sw1u[g], wv1u[g], b_w1u[g])
    b_w1d = cast_wd(w1d, sw1d, "w1d")
    b_win = cast_g256(w_in, 0, 28, swin, "win")
    b_wgt = P.buf("wgt")
    wgt_v = w_in[:, 7168:7184].rearrange("(c p) j -> p c j", p=128)
    cast_dma(swgt, wgt_v, b_wgt)
    n_pre_first = len(PRE)
    b_wout = cast_g256(w_out, 0, 8, swout, "wout")
    b_w2g = cast_g256(w2g, 0, 22, sw2g, "w2g"); b_w2u = cast_g256(w2u, 0, 22, sw2u, "w2u")
    b_w2d = cast_wd(w2d, sw2d, "w2d")
    if "A" not in phases:
        for f_ in PRE[:n_pre_first]:
            f_()
    PRE_REST = PRE[n_pre_first:]

    def emit_pre_slice(i, n):
        k = (len(PRE_REST) + n - 1) // n
        for f_ in PRE_REST[i * k:(i + 1) * k]:
            f_()

    b_X1 = P.bufs(NT * 4, "X1"); b_X2 = P.bufs(NT * 4, "X2")
    b_QT = P.bufs(8, "QT"); b_KT = P.bufs(8, "KT"); b_MQ = P.bufs(8, "MQ"); b_MK = P.bufs(8, "MK")
    b_VV = P.buf("VV"); b_MV = P.buf("MV"); b_MO = P.buf("MO"); b_GA = P.buf("GA"); b_CC = P.buf("CC")

    def ffn_phase_alloc(ngb):
        A = {}
        A["xy"] = AR.f32(4, 2048); A["b_xy"] = P.bufs(4, "xy")
        A["hT"] = AR.bf16(16, 512); A["b_hT"] = P.buf("hT")
        A["actT"] = AR.bf16(44, 512); A["b_act"] = P.bufs(44, "act")
        A["wgs"] = [AR.bf16(16, 256) for _ in range(2)]; A["b_wgs"] = P.bufs(2, "wgs")
        A["wus"] = [AR.bf16(16, 256) for _ in range(2)]; A["b_wus"] = P.bufs(2, "wus")
        A["wds"] = [AR.bf16(11, 512) for _ in range(2)]; A["b_wds"] = P.bufs(2, "wds")
        A["hn"] = [AR.bf16(2048) for _ in range(2)]; A["b_hn"] = P.bufs(2, "hn")
        A["sg"] = [AR.f32(512) for _ in range(2)]; A["b_sg"] = P.bufs(2, "sg")
        A["xr"] = [AR.f32(2048) for _ in range(2)]; A["b_xr"] = P.bufs(2, "xr")
        A["gb"] = [AR.f32(2048) for _ in range(ngb)]; A["b_gb"] = P.bufs(ngb, "gb")
        A["stg"] = [AR.bf16(512) for _ in range(3)]; A["b_stg"] = P.bufs(3, "stg")
        A["stgT"] = [AR.bf16(4, 256) for _ in range(2)]; A["b_stgT"] = P.bufs(2, "stgT")
        A["ss"] = AR.f32(4); A["b_ss"] = P.bufs(2, "ss")
        A["rstd"] = AR.f32(4); A["b_rstd"] = P.bufs(2, "rstd")
        A["gT"] = [AR.f32(16) for _ in range(2)]; A["b_gT"] = P.bufs(2, "gT")
        A["cnt"] = {"wgu": 0, "wds": 0, "stg": 0, "stgT": 0, "xr": 0, "sg": 0, "hn": 0}
        return A

    def stats_rstd(A, n_feat, const, hf):
        ss, rstd = A["ss"][:, 2 * hf:2 * hf + 2], A["rstd"][:, 2 * hf:2 * hf + 2]
        P.op("dve", lambda e: e.tensor_scalar(out=rstd, in0=ss, scalar1=1.0 / n_feat, scalar2=EPS, op0=ALU.mult, op1=ALU.add),
             reads=[A["b_ss"][hf]], writes=[A["b_rstd"][hf]])
        P.op("act", lambda e: e.activation(out=rstd, in_=rstd, func=AF.Ln), reads=[A["b_rstd"][hf]], writes=[A["b_rstd"][hf]])
        P.op("act", lambda e: e.activation(out=rstd, in_=rstd, func=AF.Exp, scale=-0.5, bias=float(math.log(const))),
             reads=[A["b_rstd"][hf]], writes=[A["b_rstd"][hf]])

    def sumsq_blocks(A, hf):
        xy = A["xy"]
        for b in (2 * hf, 2 * hf + 1):
            j = A["cnt"]["hn"] % 2
            A["cnt"]["hn"] += 1
            P.op("act", lambda e, b=b, j=j: e.activation(out=A["hn"][j], in_=xy[:, b, :], func=AF.Square, accum_out=A["ss"][:, b:b + 1]),
                 reads=[A["b_xy"][b]], writes=[A["b_hn"][j], A["b_ss"][hf]])

    def norm_transpose(A, gT, b_gT, src=None, b_src=None, scale=True, blocks=(0, 1, 2, 3)):
        hT = A["hT"]
        for b in blocks:
            if scale:
                j = A["cnt"]["hn"] % 2
                A["cnt"]["hn"] += 1
                hn, b_hn = A["hn"][j], A["b_hn"][j]
                P.op("act", lambda e, b=b, hn=hn: e.activation(out=hn, in_=A["xy"][:, b, :], func=AF.Copy, scale=A["rstd"][:, b:b + 1]),
                     reads=[A["b_xy"][b], A["b_rstd"][b // 2]], writes=[b_hn])
            else:
                hn, b_hn = src[:, b, :], b_src[b]
            for half in range(2):
                bk = next_bank()
                pb = bankbf(bk)
                for j8 in range(8):
                    c = half * 8 + j8
                    P.op("pe", lambda e, pb=pb, j8=j8, c=c, hn=hn: e.transpose(out=pb[:, j8 * 128:(j8 + 1) * 128], in_=hn[:, c * 128:(c + 1) * 128], identity=ident),
                         reads=[b_hn, b_ident], writes=[bps[bk]])
                dst = hT[:, half * 8:(half + 1) * 8, b * 128:(b + 1) * 128]
                srcp = pb[:, 0:1024].rearrange("p (a b) -> p a b", a=8)
                if gT is not None:
                    gbc = gT[:, half * 8:(half + 1) * 8].unsqueeze(2).to_broadcast([128, 8, 128])
                    P.op("dve", lambda e, dst=dst, srcp=srcp, gbc=gbc: e.tensor_tensor(out=dst, in0=srcp, in1=gbc, op=ALU.mult),
                         reads=[bps[bk], b_gT], writes=[A["b_hT"]])
                else:
                    copy_op(alt_eng(), dst, srcp, [bps[bk]], [A["b_hT"]])

    def ffn(A, swg, swu, swd, bwg, bwu, bwd):
        hT, actT = A["hT"], A["actT"]
        for g in range(22):
            s = A["cnt"]["wgu"] % 2
            A["cnt"]["wgu"] += 1
            wg_t, wu_t = A["wgs"][s], A["wus"][s]
            P.dma("sp", f"wg{s}", lambda e, g=g, wg_t=wg_t: e.dma_start(out=wg_t, in_=swg[g]), reads=[bwg[g]], writes=[A["b_wgs"][s]])
            P.dma("sp", f"wu{s}", lambda e, g=g, wu_t=wu_t: e.dma_start(out=wu_t, in_=swu[g]), reads=[bwu[g]], writes=[A["b_wus"][s]])
            for j in range(2):
                f = 2 * g + j
                bg, bu = next_bank(), next_bank()
                for c in range(16):
                    P.op("pe", lambda e, bg=bg, c=c, j=j, wg_t=wg_t: e.matmul(bank(bg), lhsT=wg_t[:, c, j * 128:(j + 1) * 128], rhs=hT[:, c, :], start=(c == 0), stop=(c == 15)),
                         reads=[A["b_wgs"][s], A["b_hT"]], writes=[bps[bg]])
                for c in range(16):
                    P.op("pe", lambda e, bu=bu, c=c, j=j, wu_t=wu_t: e.matmul(bank(bu), lhsT=wu_t[:, c, j * 128:(j + 1) * 128], rhs=hT[:, c, :], start=(c == 0), stop=(c == 15)),
                         reads=[A["b_wus"][s], A["b_hT"]], writes=[bps[bu]])
                k = A["cnt"]["sg"] % 2
                A["cnt"]["sg"] += 1
                sg = A["sg"][k]
                P.op("act", lambda e, bg=bg, sg=sg: e.activation(out=sg, in_=bank(bg), func=AF.Silu), reads=[bps[bg]], writes=[A["b_sg"][k]])
                P.op("dve", lambda e, bu=bu, sg=sg, f=f: e.tensor_tensor(out=actT[:, f, :], in0=bank(bu), in1=sg, op=ALU.mult),
                     reads=[bps[bu], A["b_sg"][k]], writes=[A["b_act"][f]])
        for n in range(4):
            for fp in range(4):
                s = A["cnt"]["wds"] % 2
                A["cnt"]["wds"] += 1
                wd_t = A["wds"][s]
                P.dma("sp", f"wd{s}", lambda e, n=n, fp=fp, wd_t=wd_t: e.dma_start(out=wd_t, in_=swd[n][:, fp * 11:(fp + 1) * 11, :]),
                      reads=[bwd[n][fp]], writes=[A["b_wds"][s]])
                for fi in range(11):
                    f = fp * 11 + fi
                    for b in range(4):
                        P.op("pe", lambda e, b=b, f=f, fi=fi, wd_t=wd_t: e.matmul(bank(4 + b), lhsT=actT[:, f, b * 128:(b + 1) * 128], rhs=wd_t[:, fi, :], start=(f == 0), stop=(f == 43)),
                             reads=[A["b_act"][f], A["b_wds"][s]], writes=[bps[4 + b]])
            for b in range(4):
                copy_op(alt_eng(), A["xy"][:, b, n * 512:(n + 1) * 512], bank(4 + b), [bps[4 + b]], [A["b_xy"][b]])

    def post_residual(A, t, res_dram, b_res, gb, b_gb, const, dst_dram, b_dst, is_final, hf):
        xy = A["xy"]
        sumsq_blocks(A, hf)
        stats_rstd(A, D, const, hf)
        for b in (2 * hf, 2 * hf + 1):
            k = A["cnt"]["xr"] % 2
            A["cnt"]["xr"] += 1
            xr = A["xr"][k]
            r0 = t * 512 + b * 128
            rd = [b_res[t * 4 + b]] if b_res is not None else []
            P.dma("sp", f"xr{k}", lambda e, xr=xr, r0=r0: e.dma_start(out=xr, in_=res_dram[r0:r0 + 128, :]), reads=rd, writes=[A["b_xr"][k]])
            P.op("dve", lambda e, b=b: e.scalar_tensor_tensor(out=xy[:, b, :], in0=xy[:, b, :], scalar=A["rstd"][:, b:b + 1], in1=gb, op0=ALU.mult, op1=ALU.mult),
                 reads=[A["b_xy"][b], A["b_rstd"][hf], b_gb], writes=[A["b_xy"][b]])
            P.op("dve", lambda e, b=b, xr=xr: e.tensor_tensor(out=xy[:, b, :], in0=xy[:, b, :], in1=xr, op=ALU.add),
                 reads=[A["b_xy"][b], A["b_xr"][k]], writes=[A["b_xy"][b]])
            wr = [b_dst[t * 4 + b]] if b_dst is not None else []
            op = P.dma("pool", f"sx{b}", lambda e, b=b, r0=r0: e.dma_start(out=dst_dram[r0:r0 + 128, :], in_=xy[:, b, :]), reads=[A["b_xy"][b]], writes=wr)
            if is_final:
                final_ops.append(op)

    if "A" in phases:
        AR.off = persist_off
        A = ffn_phase_alloc(1)
        wgt = AR.bf16(16, 16); b_wgtt = P.buf("wgtt")
        gbias = AR.f32(16); b_gbias = P.buf("gbias")
        gst = AR.f32(4, 16); b_gst = P.buf("gst")
        load_gT(A["gT"][0], gv["ffn1_pre_g"], A["b_gT"][0])
        load_gT(A["gT"][1], gv["mix_pre_g"], A["b_gT"][1])
        P.dma("sp", cs(), lambda e: e.dma_start(out=A["gb"][0], in_=bcast_row(gv["ffn1_post_g"], D)), writes=[A["b_gb"][0]])
        P.dma("sp", cs(), lambda e: e.dma_start(out=gbias, in_=bcast_row(gate_bias, 16)), writes=[b_gbias])
        FM = {}
        for g in range(4):
            FM[g] = (QT, b_QT, 2 * g); FM[4 + g] = (KT, b_KT, 2 * g)
            FM[12 + g] = (MQ, b_MQ, 2 * g); FM[16 + g] = (MK, b_MK, 2 * g)
        TM = {}
        for g in range(4):
            TM[8 + g] = (VV, b_VV, g * 256); TM[20 + g] = (MV, b_MV, g * 256); TM[24 + g] = (MO, b_MO, g * 256)
        for t in range(NT):
            xy = A["xy"]
            xsrc = x[t * 512:(t + 1) * 512, :].rearrange("(b p) d -> p b d", p=128)
            P.dma("sp", "xl", lambda e, xsrc=xsrc: e.dma_start(out=xy, in_=xsrc), writes=A["b_xy"])
            if t == 0:
                pre_gate.extend(A["b_xy"])
                for f_ in PRE[:8]:
                    f_()
                del pre_gate[:]
                for f_ in PRE[8:n_pre_first]:
                    f_()
            for hf in range(2):
                sumsq_blocks(A, hf)
                stats_rstd(A, D, 1.0, hf)
                norm_transpose(A, A["gT"][0], A["b_gT"][0], blocks=(2 * hf, 2 * hf + 1))
            ffn(A, sw1g, sw1u, sw1d, b_w1g, b_w1u, b_w1d)
            for hf in range(2):
                post_residual(A, t, x, None, A["gb"][0], A["b_gb"][0], 0.5, X1, b_X1, False, hf)
                sumsq_blocks(A, hf)
                stats_rstd(A, D, 1.0, hf)
                norm_transpose(A, A["gT"][1], A["b_gT"][1], blocks=(2 * hf, 2 * hf + 1))
            hT = A["hT"]
            for g in range(28):
                s = A["cnt"]["wgu"] % 2
                A["cnt"]["wgu"] += 1
                w_t = A["wgs"][s]
                P.dma("sp", f"wg{s}", lambda e, g=g, w_t=w_t: e.dma_start(out=w_t, in_=swin[g]), reads=[b_win[g]], writes=[A["b_wgs"][s]])
                if g in FM:
                    dst, bdst, ch0 = FM[g]
                    for j in range(2):
                        bk = next_bank()
                        for c in range(16):
                            P.op("pe", lambda e, bk=bk, c=c, j=j, w_t=w_t: e.matmul(bank(bk), lhsT=w_t[:, c, j * 128:(j + 1) * 128], rhs=hT[:, c, :], start=(c == 0), stop=(c == 15)),
                                 reads=[A["b_wgs"][s], A["b_hT"]], writes=[bps[bk]])
                        k = A["cnt"]["stg"] % 3
                        A["cnt"]["stg"] += 1
                        stg = A["stg"][k]
                        copy_op(alt_eng(), stg, bank(bk), [bps[bk]], [A["b_stg"][k]])
                        ch = ch0 + j
                        P.dma("pool", f"sg{k}", lambda e, dst=dst, ch=ch, stg=stg, t=t: e.dma_start(out=dst[ch][:, t * 512:(t + 1) * 512], in_=stg),
                              reads=[A["b_stg"][k]], writes=[bdst[ch]])
                else:
                    dst, bdst, col0 = TM[g]
                    k = A["cnt"]["stgT"] % 2
                    A["cnt"]["stgT"] += 1
                    stgT = A["stgT"][k]
                    for b in range(4):
                        bk = next_bank()
                        for c in range(16):
                            P.op("pe", lambda e, bk=bk, c=c, b=b, w_t=w_t: e.matmul(bank(bk)[:, 0:256], lhsT=hT[:, c, b * 128:(b + 1) * 128], rhs=w_t[:, c, :], start=(c == 0), stop=(c == 15)),
                                 reads=[A["b_wgs"][s], A["b_hT"]], writes=[bps[bk]])
                        copy_op(alt_eng(), stgT[:, b, :], bank(bk)[:, 0:256], [bps[bk]], [A["b_stgT"][k]])
                    dv = dst[t * 512:(t + 1) * 512, col0:col0 + 256].rearrange("(b p) j -> p b j", p=128)
                    P.dma("pool", f"sT{k}", lambda e, dv=dv, stgT=stgT: e.dma_start(out=dv, in_=stgT), reads=[A["b_stgT"][k]], writes=[bdst])
            if t == 0:
                P.dma("sp", cs(), lambda e: e.dma_start(out=wgt, in_=swgt), reads=[b_wgt], writes=[b_wgtt])
            for b in range(4):
                bk = next_bank()
                for c in range(16):
                    P.op("pe", lambda e, bk=bk, c=c, b=b: e.matmul(bank(bk)[:, 0:16], lhsT=hT[:, c, b * 128:(b + 1) * 128], rhs=wgt[:, c, :], start=(c == 0), stop=(c == 15)),
                         reads=[b_wgtt, A["b_hT"]], writes=[bps[bk]])
                P.op("dve", lambda e, bk=bk, b=b: e.tensor_tensor(out=gst[:, b, :], in0=bank(bk)[:, 0:16], in1=gbias, op=ALU.add),
                     reads=[bps[bk], b_gbias], writes=[b_gst])
            gdv = GATES[t * 512:(t + 1) * 512, :].rearrange("(b p) j -> p b j", p=128)
            P.dma("pool", "sG", lambda e, gdv=gdv: e.dma_start(out=gdv, in_=gst), reads=[b_gst], writes=[b_GA])
            emit_pre_slice(t, NT)
        P.barrier(barscr)

    if "A" not in phases:
        emit_pre_slice(0, 1)

    if "B" in phases:
        AR.off = persist_off
        qT = [AR.bf16(S) for _ in range(2)]; b_qT = P.bufs(2, "qT")
        kT = [AR.bf16(S) for _ in range(2)]; b_kT = P.bufs(2, "kT")
        vA = [AR.bf16(32, 129) for _ in range(2)]; b_vA = P.bufs(2, "vA")
        ET = AR.f32(48, 128); b_ET = P.buf("ET")
        HR = AR.f32(48, 128); b_HR = P.buf("HR")
        cm = AR.f32(16); cp = AR.f32(16); b_cc = P.buf("cmcp")
        agb = AR.f32(1024); b_agb = P.buf("agb")
        lam4 = AR.f32(4, 64); b_lam4 = P.buf("lam4")
        lprod = AR.f32(2, 64); lsum = AR.f32(2); nlam = AR.f32(1); b_lam = P.buf("lam")
        pT = [AR.bf16(2, 512) for _ in range(3)]; b_pT = P.bufs(3, "pT")
        ocp = AR.f32(4, 129); b_ocp = P.buf("ocp")
        vP = [[AR.bf16(32, 129) for _ in range(2)] for _ in range(2)]; b_vP = [P.bufs(2, f"vP{i}") for i in range(2)]
        cdm = AR.f32(16); ecd = AR.f32(16); b_ecd = P.buf("ecd")
        ofall2 = [AR.f32(32, 128) for _ in range(2)]; b_ofall2 = P.bufs(2, "ofall")
        osall = AR.bf16(32, 128); b_osall = P.buf("osall")
        rz = AR.f32(2, 2); b_rz = P.bufs(2, "rz")
        ssH = AR.f32(32); rsH = AR.f32(32); b_ssH = P.buf("ssH")
        rb_sb = AR.f32(16); ohs = AR.f32(512); fsb = AR.f32(512); b_rb = P.buf("rb"); b_oh = P.buf("oh"); b_fsb = P.buf("fsb"); b_FS = P.buf("FS")
        P.dma("sp", cs(), lambda e: e.dma_start(out=rb_sb[0:32, :], in_=relb), writes=[b_rb])
        P.dma("sp", cs(), lambda e: e.dma_start(out=ohs[0:32, :], in_=c_oh), writes=[b_oh])
        P.op("pe", lambda e: e.matmul(bank(0)[0:16, :], lhsT=rb_sb[0:32, :], rhs=ohs[0:32, :], start=True, stop=True), reads=[b_rb, b_oh], writes=[bps[0]])
        P.op("dve", lambda e: e.tensor_copy(out=fsb[0:16, :], in_=bank(0)[0:16, :]), reads=[bps[0]], writes=[b_fsb])
        P.dma("sp", cs(), lambda e: e.dma_start(out=FSCR, in_=fsb[0:16, :]), reads=[b_fsb], writes=[b_FS])
        for col in range(16):
            srcH = bass.AP(FSCR.tensor, FSCR.offset + col * 512, [[1, 128], [128, 3], [1, 128]])
            P.dma("sp", cs(), lambda e, col=col, srcH=srcH: e.dma_start(out=HR[:, col * 3:(col + 1) * 3, :], in_=srcH), reads=[b_FS], writes=[b_HR])
        hra = HR
        rev = bass.AP(hra.tensor, hra.offset + 127, [list(hra.ap[0]), [128, 48], [-1, 128]])
        P.dma("sp", cs(), lambda e: e.dma_start(out=cm, in_=bcast_row(relb, 16, 15 * 16)), writes=[b_cc])
        P.dma("sp", cs(), lambda e: e.dma_start(out=cp, in_=bcast_row(relb, 16, 31 * 16)), writes=[b_cc])
        P.op("dve", lambda e: e.tensor_copy(out=ET, in_=rev), reads=[b_HR], writes=[b_ET])
        for col in range(16):
            P.op("dve", lambda e, col=col: e.tensor_scalar(out=ET[:, col * 3:(col + 1) * 3, :], in0=ET[:, col * 3:(col + 1) * 3, :], scalar1=cm[:, col:col + 1], scalar2=None, op0=ALU.subtract),
                 reads=[b_ET, b_cc], writes=[b_ET])
        P.op("act", lambda e: e.activation(out=ET, in_=ET, func=AF.Exp), reads=[b_ET], writes=[b_ET])
        P.op("dve", lambda e: e.tensor_tensor(out=cdm, in0=cp, in1=cm, op=ALU.subtract), reads=[b_cc], writes=[b_ecd])
        P.op("act", lambda e: e.activation(out=ecd, in_=cdm, func=AF.Exp), reads=[b_ecd], writes=[b_ecd])
        P.dma("sp", cs(), lambda e: e.dma_start(out=agb, in_=bcast_row(att_g, 1024)), writes=[b_agb])
        for i, la in enumerate((lq1, lk1, lq2, lk2)):
            P.dma("sp", cs(), lambda e, i=i, la=la: e.dma_start(out=lam4[:, i, :], in_=bcast_row(la, 64)), writes=[b_lam4])
        P.op("dve", lambda e: e.tensor_tensor(out=lprod[:, 0, :], in0=lam4[:, 0, :], in1=lam4[:, 1, :], op=ALU.mult), reads=[b_lam4], writes=[b_lam])
        P.op("dve", lambda e: e.tensor_tensor(out=lprod[:, 1, :], in0=lam4[:, 2, :], in1=lam4[:, 3, :], op=ALU.mult), reads=[b_lam4, b_lam], writes=[b_lam])
        P.op("dve", lambda e: e.tensor_reduce(out=lsum, in_=lprod, axis=AX.X, op=ALU.add), reads=[b_lam], writes=[b_lam])
        P.op("act", lambda e: e.activation(out=lsum, in_=lsum, func=AF.Exp), reads=[b_lam], writes=[b_lam])
        P.op("dve", lambda e: e.tensor_tensor(out=nlam, in0=lsum[:, 1:2], in1=lsum[:, 0:1], op=ALU.subtract), reads=[b_lam], writes=[b_lam])
        P.op("dve", lambda e: e.tensor_single_scalar(out=nlam, in_=nlam, scalar=-LAMBDA_INIT, op=ALU.add), reads=[b_lam], writes=[b_lam])
        for hb in range(2):
            P.op("pool", lambda e, hb=hb: e.memset(vA[hb][:, :, 128:129], 1.0), writes=[b_vA[hb]])
        cnt = {"pT": 0, "etmp": 0, "sb": 0, "oS": 0}
        import os as _os
        NH = int(_os.environ.get('K_DBG_HEADS', '8'))

        def issue_loads(h):
            hb = h % 2
            P.dma("sp", f"aq{hb}", lambda e, h=h, hb=hb: e.dma_start(out=qT[hb], in_=QT[h]), reads=[b_QT[h]], writes=[b_qT[hb]])
            P.dma("sp", f"ak{hb}", lambda e, h=h, hb=hb: e.dma_start(out=kT[hb], in_=KT[h]), reads=[b_KT[h]], writes=[b_kT[hb]])
            vsrc = VV[:, h * 128:(h + 1) * 128].rearrange("(c p) e -> p c e", p=128)
            P.dma("sp", f"av{hb}", lambda e, hb=hb, vsrc=vsrc: e.dma_start(out=vA[hb][:, :, 0:128], in_=vsrc), reads=[b_VV], writes=[b_vA[hb]])

        def compute_vP(h):
            hb = h % 2
            for m in range(2):
                P.op("dve", lambda e, m=m, hb=hb, h=h: e.tensor_scalar(out=vP[hb][m], in0=vA[hb], scalar1=ecd[:, 2 * h + m:2 * h + m + 1], scalar2=None, op0=ALU.mult),
                     reads=[b_vA[hb], b_ecd], writes=[b_vP[hb][m]])

        pending_head_end = []
        if NH > 0:
            issue_loads(0)
            compute_vP(0)
        for h in range(NH):
            hb = h % 2
            if h + 1 < NH:
                issue_loads(h + 1)
            q_, k_, v_ = qT[hb], kT[hb], vA[hb]
            ofall, b_ofall = ofall2[hb], b_ofall2[hb]
            NQT = int(_os.environ.get('K_DBG_QT', '16'))

            def emit_qk(it, q_=q_, k_=k_, hb=hb):
                qt_, pr_ = divmod(it, 16)
                sb_ = it % 2
                for kk in range(2):
                    kb_ = 2 * pr_ + kk
                    for m in range(2):
                        P.op("pe", lambda e, sb_=sb_, m=m, kb_=kb_, qt_=qt_, kk=kk: e.matmul(
                            bank(2 * sb_ + m)[:, kk * 256:(kk + 1) * 256], lhsT=k_[m * 64:(m + 1) * 64, kb_ * 128:(kb_ + 1) * 128],
                            rhs=q_[m * 64:(m + 1) * 64, qt_ * 256:qt_ * 256 + 256], start=True, stop=True),
                            reads=[b_qT[hb], b_kT[hb]], writes=[bps[2 * sb_ + m]])

            emit_qk(0)
            if NQT * 16 > 1:
                emit_qk(1)
            for qt in range(NQT):
                if qt == 2 and h + 1 < NH:
                    compute_vP(h + 1)
                if qt == 1 and pending_head_end:
                    pending_head_end.pop(0)()
                for pr in range(16):
                    it = qt * 16 + pr
                    sb_ = it % 2
                    pk = cnt["pT"] % 3
                    cnt["pT"] += 1
                    p_t = pT[pk]
                    ty = {}
                    for kk in range(2):
                        for i in range(2):
                            off = (2 * pr + kk) - (2 * qt + i)
                            ty[(kk, i)] = "m" if off <= -2 else ("p" if off >= 2 else off)
                    src = ps[:, 2 * sb_:2 * sb_ + 2, :]
                    P.op("act", lambda e, src=src, p_t=p_t: e.activation(out=p_t, in_=src, func=AF.Exp, scale=0.125),
                         reads=[bps[2 * sb_], bps[2 * sb_ + 1]], writes=[b_pT[pk]])
                    for kk in range(2):
                        for i in range(2):
                            c0 = kk * 256 + i * 128
                            t_ = ty[(kk, i)]
                            if t_ not in ("m", "p"):
                                for m in range(2):
                                    eti = ET[:, (2 * h + m) * 3 + (t_ + 1), :]
                                    P.op("dve", lambda e, p_t=p_t, c0=c0, m=m, eti=eti: e.tensor_tensor(out=p_t[:, m, c0:c0 + 128], in0=p_t[:, m, c0:c0 + 128], in1=eti, op=ALU.mult),
                                         reads=[b_pT[pk], b_ET], writes=[b_pT[pk]])
                    if it + 2 < NQT * 16:
                        emit_qk(it + 2)
                    for kk in range(2):
                        kb = 2 * pr + kk
                        for i in range(2):
                            for m in range(2):
                                if ty[(kk, i)] == "p":
                                    vv, bvv = vP[hb][m], b_vP[hb][m]
                                else:
                                    vv, bvv = v_, b_vA[hb]
                                P.op("pe", lambda e, i=i, m=m, p_t=p_t, kb=kb, vv=vv, kk=kk: e.matmul(
                                    bank(4 + 2 * i + m)[:, 0:129], lhsT=p_t[:, m, kk * 256 + i * 128:kk * 256 + (i + 1) * 128], rhs=vv[:, kb, :], start=(kb == 0), stop=(kb == 31)),
                                    reads=[b_pT[pk], bvv], writes=[bps[4 + 2 * i + m]])
                P.op("dve", lambda e: e.tensor_copy(out=ocp, in_=ps[:, 4:8, 0:129]), reads=[bps[4], bps[5], bps[6], bps[7]], writes=[b_ocp])
                for i in range(2):
                    blk = 2 * qt + i
                    P.op("dve", lambda e, i=i: e.reciprocal(out=rz[:, i, :], in_=ocp[:, 2 * i:2 * i + 2, 128]), reads=[b_ocp], writes=[b_rz[i]])
                    P.op("dve", lambda e, i=i: e.tensor_tensor(out=rz[:, i, 1:2], in0=rz[:, i, 1:2], in1=nlam, op=ALU.mult), reads=[b_rz[i], b_lam], writes=[b_rz[i]])
                    P.op("dve", lambda e, i=i, blk=blk, ofall=ofall: e.tensor_scalar(out=ofall[:, blk, :], in0=ocp[:, 2 * i, 0:128], scalar1=rz[:, i, 0:1], scalar2=None, op0=ALU.mult),
                         reads=[b_ocp, b_rz[i]], writes=[b_ofall])
                    P.op("dve", lambda e, i=i, blk=blk, ofall=ofall: e.scalar_tensor_tensor(out=ofall[:, blk, :], in0=ocp[:, 2 * i + 1, 0:128], scalar=rz[:, i, 1:2], in1=ofall[:, blk, :], op0=ALU.mult, op1=ALU.add),
                         reads=[b_ocp, b_rz[i], b_ofall], writes=[b_ofall])
            def head_end(h=h, ofall=ofall, b_ofall=b_ofall, NQT=NQT):
                sqv = HR[:, 0:32, :]
                nblk = 2 * NQT
                P.op("dve", lambda e, sqv=sqv: e.tensor_tensor(out=sqv, in0=ofall, in1=ofall, op=ALU.mult), reads=[b_ofall], writes=[b_HR])
                P.op("dve", lambda e, sqv=sqv: e.tensor_reduce(out=ssH, in_=sqv, axis=AX.X, op=ALU.add), reads=[b_HR], writes=[b_ssH])
                P.op("dve", lambda e: e.tensor_scalar(out=rsH, in0=ssH, scalar1=1.0 / 128, scalar2=EPS, op0=ALU.mult, op1=ALU.add), reads=[b_ssH], writes=[b_ssH])
                P.op("act", lambda e: e.activation(out=rsH, in_=rsH, func=AF.Ln), reads=[b_ssH], writes=[b_ssH])
                P.op("act", lambda e: e.activation(out=rsH, in_=rsH, func=AF.Exp, scale=-0.5, bias=float(math.log(1.0 - LAMBDA_INIT))), reads=[b_ssH], writes=[b_ssH])
                P.op("dve", lambda e: e.tensor_tensor(out=ofall, in0=ofall, in1=rsH.unsqueeze(2).to_broadcast([128, 32, 128]), op=ALU.mult), reads=[b_ofall, b_ssH], writes=[b_ofall])
                ag_bc = bass.AP(agb.tensor, agb.offset + h * 128, [list(agb.ap[0]), [0, 32], [1, 128]])
                P.op("dve", lambda e, ag_bc=ag_bc: e.tensor_tensor(out=osall, in0=ofall, in1=ag_bc, op=ALU.mult), reads=[b_ofall, b_agb], writes=[b_osall])
                cdv = CONCAT[:, h * 128:(h + 1) * 128].rearrange("(c p) e -> p c e", p=128)
                P.dma("pool", "so0", lambda e, cdv=cdv: e.dma_start(out=cdv, in_=osall), reads=[b_osall], writes=[b_CC])

            pending_head_end.append(head_end)
        while pending_head_end:
            pending_head_end.pop(0)()
        P.barrier(barscr)

    if "C" in phases:
        AR.off = persist_off
        trif = AR.f32(128); trib = AR.f32(128); onesf = AR.f32(128); b_tri = P.buf("tri")
        P.dma("sp", cs(), lambda e: e.dma_start(out=trif, in_=c_trif), writes=[b_tri])
        P.dma("sp", cs(), lambda e: e.dma_start(out=trib, in_=c_trib), writes=[b_tri])
        P.dma("sp", cs(), lambda e: e.dma_start(out=onesf, in_=c_ones), writes=[b_tri])
        gt = AR.f32(32, 16); b_gt = P.buf("gt")
        lfn = AR.f32(32, 8); b_lfn = P.buf("lfn")
        nb = AR.f32(32, 8); nbl = AR.f32(32, 8); b_nb = P.buf("nb")
        alpha = AR.f32(32, 8); beta = AR.f32(32, 8); decay = AR.f32(32, 8); b_abd = P.buf("abd")
        cw = AR.f32(5, 16); cb = AR.f32(16); b_cw = P.buf("cw")
        mgb = AR.f32(1024); b_mgb = P.buf("mgb")
        P.dma("sp", cs(), lambda e: e.dma_start(out=gt, in_=GATES.rearrange("(c p) g -> p c g", p=128)), reads=[b_GA], writes=[b_gt])
        for j in range(5):
            srcw = conv_w[j:j + 1, :].rearrange("o (c p) -> p (o c)", p=128)
            P.dma("sp", cs(), lambda e, j=j, srcw=srcw: e.dma_start(out=cw[:, j, :], in_=srcw, allow_slow_non_contiguous=True), writes=[b_cw])
        load_gT(cb, conv_b, b_cw)
        P.dma("sp", cs(), lambda e: e.dma_start(out=mgb, in_=bcast_row(ml_g, 1024)), writes=[b_mgb])
        P.op("act", lambda e: e.activation(out=lfn, in_=gt[:, :, 8:16], func=AF.Exp, scale=-1.0), reads=[b_gt], writes=[b_lfn])
        P.op("act", lambda e: e.activation(out=lfn, in_=lfn, func=AF.Ln, bias=1.0), reads=[b_lfn], writes=[b_lfn])
        P.op("pe", lambda e: e.matmul(bank(0)[:, 0:128], lhsT=trif, rhs=lfn[:, :, 0:4], start=True, stop=True), reads=[b_tri, b_lfn], writes=[bps[0]])
        P.op("pe", lambda e: e.matmul(bank(0)[:, 128:256], lhsT=trib, rhs=lfn[:, :, 4:8], start=True, stop=True), reads=[b_tri, b_lfn], writes=[bps[0]])
        P.op("pe", lambda e: e.matmul(bank(1)[:, 0:256], lhsT=onesf, rhs=lfn, start=True, stop=True), reads=[b_tri, b_lfn], writes=[bps[1]])
        for d_ in range(2):
            P.op("dve", lambda e, d_=d_: e.tensor_copy(out=nb[:, :, d_ * 4:(d_ + 1) * 4], in_=bank(0)[:, d_ * 128:(d_ + 1) * 128].rearrange("p (c h) -> p c h", h=4)),
                 reads=[bps[0]], writes=[b_nb])
        P.op("dve", lambda e: e.tensor_copy(out=nbl, in_=bank(1)[:, 0:256].rearrange("p (c h) -> p c h", h=8)), reads=[bps[1]], writes=[b_nb])
        P.op("act", lambda e: e.activation(out=alpha, in_=nb, func=AF.Exp, scale=-1.0, bias=float(-math.log(16.0))), reads=[b_nb], writes=[b_abd])
        P.op("dve", lambda e: e.tensor_tensor(out=beta, in0=gt[:, :, 0:8], in1=nb, op=ALU.add), reads=[b_gt, b_nb], writes=[b_abd])
        P.op("act", lambda e: e.activation(out=beta, in_=beta, func=AF.Exp), reads=[b_abd], writes=[b_abd])
        P.op("act", lambda e: e.activation(out=decay, in_=nbl, func=AF.Exp, scale=-1.0), reads=[b_nb], writes=[b_abd])

        qTm = AR.bf16(2, S); kTm = AR.bf16(2, S); b_qTm = P.bufs(2, "qTm"); b_kTm = P.bufs(2, "kTm")
        raw = [AR.bf16(1028) for _ in range(2)]; b_raw = P.bufs(2, "raw")
        cacc = [AR.f32(1024) for _ in range(2)]; b_cacc = P.bufs(2, "cacc")
        kTok = AR.bf16(32, 256); b_kTok = P.buf("kTok")
        vt = AR.bf16(32, 257); b_vt = P.buf("vt")
        rawo = [AR.f32(32, 257) for _ in range(2)]; b_ro = [P.bufs(32, f"ro{i}") for i in range(2)]
        hm = rawo[0][:, :, 0:256]; hm1 = rawo[1][:, :, 0:256]
        b_hm = b_ro[0]
        dnb = AR.f32(4, 32); rrb = AR.f32(2, 32); b_dnb = P.buf("dnb")
        mo = AR.bf16(32, 256); b_mo = P.buf("mo")
        sq = AR.f32(8, 256); b_sq = P.buf("sq")
        ssm = AR.f32(32); rsm = AR.f32(32); b_ssm = P.buf("ssm")
        CT = [AR.f32(2, 257) for _ in range(2)]; b_CT = P.bufs(2, "CT")
        CTb = [AR.bf16(2, 257) for _ in range(2)]; b_CTb = P.bufs(2, "CTb")
        ctmp = [AR.f32(257) for _ in range(2)]; b_ctmp = P.bufs(2, "ctmp")
        vaug = [AR.bf16(257) for _ in range(4)]; b_vaug = P.bufs(4, "vaug")
        Sm = [AR.bf16(128) for _ in range(4)]; b_Sm = P.bufs(4, "Sm")
        dn = [AR.f32(4) for _ in range(4)]; b_dn = P.bufs(4, "dn")
        for r in raw:
            pass
        P.op("pool", lambda e: e.memset(raw[0][:, 0:1028], 0.0), writes=[b_raw[0]])
        P.op("pool", lambda e: e.memset(raw[1][:, 0:1028], 0.0), writes=[b_raw[1]])
        P.op("pool", lambda e: e.memset(vt[:, :, 256:257], 1.0), writes=[b_vt])
        cnt = {"raw": 0, "sb": 0, "va": 0, "ct": 0}
        for hd in range(4):
            for (src_d, b_src, dstT, b_dstT, qk) in ((MQ, b_MQ, qTm, b_qTm, 0), (MK, b_MK, kTm, b_kTm, 1)):
                for dc in range(2):
                    ch = hd * 2 + dc
                    cch = qk * 8 + ch
                    for pc in range(4):
                        t0 = pc * 1024
                        k = cnt["raw"] % 2
                        cnt["raw"] += 1
                        rw, ca = raw[k], cacc[k]
                        lo = max(t0 - 2, 0); hi = min(t0 + 1026, S)
                        dlo = lo - (t0 - 2)
                        if pc == 0 or pc == 3:
                            P.op("dve", lambda e, rw=rw: e.memset(rw[:, 0:1028], 0.0), writes=[b_raw[k]])
                        P.dma("sp", f"rw{k}", lambda e, rw=rw, src_d=src_d, ch=ch, lo=lo, hi=hi, dlo=dlo: e.dma_start(out=rw[:, dlo:dlo + (hi - lo)], in_=src_d[ch][:, lo:hi]),
                              reads=[b_src[ch]], writes=[b_raw[k]])
                        P.op("dve", lambda e, rw=rw, ca=ca, cch=cch: e.tensor_scalar(out=ca, in0=rw[:, 0:1024], scalar1=cw[:, 0, cch:cch + 1], scalar2=None, op0=ALU.mult),
                             reads=[b_raw[k], b_cw], writes=[b_cacc[k]])
                        for j in range(1, 5):
                            P.op("dve", lambda e, rw=rw, ca=ca, cch=cch, j=j: e.scalar_tensor_tensor(out=ca, in0=rw[:, j:j + 1024], scalar=cw[:, j, cch:cch + 1], in1=ca, op0=ALU.mult, op1=ALU.add),
                                 reads=[b_raw[k], b_cw, b_cacc[k]], writes=[b_cacc[k]])
                        P.op("act", lambda e, ca=ca, cch=cch, dstT=dstT, dc=dc, t0=t0: e.activation(out=dstT[:, dc, t0:t0 + 1024], in_=ca, func=AF.Silu, bias=cb[:, cch:cch + 1]),
                             reads=[b_cacc[k], b_cw], writes=[b_dstT[dc]])
            for c4 in range(8):
                bk = cnt["sb"] % 4
                cnt["sb"] += 1
                pb = bankbf(bk)
                for cc_ in range(4):
                    c = c4 * 4 + cc_
                    for dc in range(2):
                        P.op("pe", lambda e, pb=pb, cc_=cc_, dc=dc, c=c: e.transpose(out=pb[:, (cc_ * 2 + dc) * 128:(cc_ * 2 + dc + 1) * 128], in_=kTm[:, dc, c * 128:(c + 1) * 128], identity=ident),
                             reads=[b_kTm[dc], b_ident], writes=[bps[bk]])
                copy_op(alt_eng(), kTok[:, c4 * 4:(c4 + 1) * 4, :], pb[:, 0:1024].rearrange("p (a b) -> p a b", a=4), [bps[bk]], [b_kTok])
            vsrc = MV[:, hd * 256:(hd + 1) * 256].rearrange("(c p) e -> p c e", p=128)
            P.dma("sp", "mv", lambda e, vsrc=vsrc: e.dma_start(out=vt[:, :, 0:256], in_=vsrc), reads=[b_MV], writes=[b_vt])
            osrc = MO[:, hd * 256:(hd + 1) * 256].rearrange("(c p) e -> p c e", p=128)
            P.dma("sp", "mo", lambda e, osrc=osrc: e.dma_start(out=mo, in_=osrc), reads=[b_MO], writes=[b_mo])
            def step_info(step):
                out = []
                for dr in range(2):
                    c = step if dr == 0 else 31 - step
                    k = (2 * step + dr) % 4
                    out.append((dr, c, dr * 4 + hd, k))
                return out

            def emit_vaug(step):
                for (dr, c, gi, k) in step_info(step):
                    P.op("act", lambda e, k=k, c=c, gi=gi: e.activation(out=vaug[k], in_=vt[:, c, :], func=AF.Copy, scale=beta[:, c, gi:gi + 1]),
                         reads=[b_vt, b_abd], writes=[b_vaug[k]])

            def emit_scores(step):
                inf = step_info(step)
                for (dr, c, gi, k) in inf:
                    for dc in range(2):
                        P.op("pe", lambda e, dr=dr, dc=dc, c=c: e.matmul(bank(dr)[:, 0:128], lhsT=kTm[:, dc, c * 128:(c + 1) * 128], rhs=qTm[:, dc, c * 128:(c + 1) * 128], start=(dc == 0), stop=(dc == 1)),
                             reads=[b_kTm[0], b_kTm[1], b_qTm[0], b_qTm[1]], writes=[bps[dr]])
                for (dr, c, gi, k) in inf:
                    msk = trif if dr == 0 else trib
                    P.op("dve", lambda e, k=k, dr=dr, msk=msk: e.tensor_tensor(out=Sm[k], in0=bank(dr)[:, 0:128], in1=msk, op=ALU.mult),
                         reads=[bps[dr], b_tri], writes=[b_Sm[k]])

            emit_vaug(0)
            emit_scores(0)
            for step in range(32):
                inf = step_info(step)
                first = (step == 0)
                for (dr, c, gi, k) in inf:
                    obk = 4 + dr
                    P.op("pe", lambda e, obk=obk, k=k, first=first: e.matmul(bank(obk)[:, 0:257], lhsT=Sm[k], rhs=vaug[k], start=True, stop=first),
                         reads=[b_Sm[k], b_vaug[k]], writes=[bps[obk]])
                    if not first:
                        for dc in range(2):
                            P.op("pe", lambda e, obk=obk, dc=dc, c=c, dr=dr: e.matmul(bank(obk)[:, 0:257], lhsT=qTm[:, dc, c * 128:(c + 1) * 128], rhs=CTb[dr][:, dc, :], start=False, stop=(dc == 1)),
                                 reads=[b_qTm[dc], b_CTb[dr]], writes=[bps[obk]])
                if step < 31:
                    for (dr, c, gi, k) in inf:
                        for dc in range(2):
                            ubk = (2 + dc) if dr == 0 else (6 + dc)
                            P.op("pe", lambda e, ubk=ubk, dc=dc, c=c, k=k: e.matmul(bank(ubk)[:, 0:257], lhsT=kTok[:, c, dc * 128:(dc + 1) * 128], rhs=vaug[k], start=True, stop=True),
                                 reads=[b_kTok, b_vaug[k]], writes=[bps[ubk]])
                for (dr, c, gi, k) in inf:
                    obk = 4 + dr
                    P.op("dve", lambda e, c=c, obk=obk, dr=dr: e.tensor_copy(out=rawo[dr][:, c, :], in_=bank(obk)[:, 0:257]),
                         reads=[bps[obk]], writes=[b_ro[dr][c]])
                if step < 31:
                    emit_vaug(step + 1)
                    for (dr, c, gi, k) in inf:
                        cprev = c - 1 if dr == 0 else c + 1
                        for dc in range(2):
                            ubk = (2 + dc) if dr == 0 else (6 + dc)
                            if first:
                                P.op("dve", lambda e, ubk=ubk, dc=dc, dr=dr: e.tensor_copy(out=CT[dr][:, dc, :], in_=bank(ubk)[:, 0:257]),
                                     reads=[bps[ubk]], writes=[b_CT[dr]])
                            else:
                                P.op("dve", lambda e, ubk=ubk, dc=dc, dr=dr, cprev=cprev, gi=gi: e.scalar_tensor_tensor(out=CT[dr][:, dc, :], in0=CT[dr][:, dc, :], scalar=decay[:, cprev, gi:gi + 1], in1=bank(ubk)[:, 0:257], op0=ALU.mult, op1=ALU.add),
                                     reads=[bps[ubk], b_CT[dr], b_abd], writes=[b_CT[dr]])
                    for (dr, c, gi, k) in inf:
                        for dc in range(2):
                            P.op("act", lambda e, dc=dc, dr=dr, c=c, gi=gi: e.activation(out=CTb[dr][:, dc, :], in_=CT[dr][:, dc, :], func=AF.Copy, scale=decay[:, c, gi:gi + 1]),
                                 reads=[b_CT[dr], b_abd], writes=[b_CTb[dr]])
                    emit_scores(step + 1)
            allro = b_ro[0] + b_ro[1]
            for dr in range(2):
                gi = dr * 4 + hd
                al = alpha[:, :, gi]
                d0, d1, d2 = dnb[:, 0, :], dnb[:, 1, :], dnb[:, 2, :]
                P.op("dve", lambda e, dr=dr, al=al, d0=d0: e.tensor_tensor(out=d0, in0=rawo[dr][:, :, 256], in1=al, op=ALU.mult), reads=b_ro[dr] + [b_abd, b_dnb], writes=[b_dnb])
                P.op("dve", lambda e, d0=d0, d1=d1: e.tensor_single_scalar(out=d1, in_=d0, scalar=-1.0, op=ALU.mult), reads=[b_dnb], writes=[b_dnb])
                P.op("dve", lambda e, d0=d0, d1=d1, d2=d2: e.scalar_tensor_tensor(out=d2, in0=d0, scalar=1.0, in1=d1, op0=ALU.max, op1=ALU.max), reads=[b_dnb], writes=[b_dnb])
                P.op("dve", lambda e, d2=d2, dr=dr: e.reciprocal(out=rrb[:, dr, :], in_=d2), reads=[b_dnb], writes=[b_dnb])
                P.op("dve", lambda e, dr=dr, al=al: e.tensor_tensor(out=rrb[:, dr, :], in0=rrb[:, dr, :], in1=al, op=ALU.mult), reads=[b_dnb, b_abd], writes=[b_dnb])
            P.op("dve", lambda e: e.tensor_tensor(out=hm, in0=hm, in1=rrb[:, 0, :].unsqueeze(2).to_broadcast([128, 32, 256]), op=ALU.mult), reads=b_ro[0] + [b_dnb], writes=b_ro[0])
            P.op("dve", lambda e: e.tensor_tensor(out=hm1, in0=hm1, in1=rrb[:, 1, :].unsqueeze(2).to_broadcast([128, 32, 256]), op=ALU.mult), reads=b_ro[1] + [b_dnb], writes=b_ro[1])
            P.op("dve", lambda e: e.tensor_tensor(out=hm, in0=hm, in1=hm1, op=ALU.add), reads=allro, writes=b_ro[0])
            for pc in range(4):
                P.op("dve", lambda e, pc=pc: e.tensor_tensor(out=sq, in0=hm[:, pc * 8:(pc + 1) * 8, :], in1=hm[:, pc * 8:(pc + 1) * 8, :], op=ALU.mult),
                     reads=b_hm[pc * 8:(pc + 1) * 8], writes=[b_sq])
                P.op("dve", lambda e, pc=pc: e.tensor_reduce(out=ssm[:, pc * 8:(pc + 1) * 8], in_=sq, axis=AX.X, op=ALU.add), reads=[b_sq], writes=[b_ssm])
            P.op("dve", lambda e: e.tensor_scalar(out=rsm, in0=ssm, scalar1=1.0 / 256, scalar2=EPS, op0=ALU.mult, op1=ALU.add), reads=[b_ssm], writes=[b_ssm])
            P.op("act", lambda e: e.activation(out=rsm, in_=rsm, func=AF.Ln), reads=[b_ssm], writes=[b_ssm])
            P.op("act", lambda e: e.activation(out=rsm, in_=rsm, func=AF.Exp, scale=-0.5), reads=[b_ssm], writes=[b_ssm])
            P.op("act", lambda e: e.activation(out=mo, in_=mo, func=AF.Sigmoid), reads=[b_mo], writes=[b_mo])
            P.op("dve", lambda e: e.tensor_tensor(out=hm, in0=hm, in1=rsm.unsqueeze(2).to_broadcast([128, 32, 256]), op=ALU.mult), reads=b_hm + [b_ssm], writes=b_hm)
            mg_bc = bass.AP(mgb.tensor, mgb.offset + hd * 256, [list(mgb.ap[0]), [0, 32], [1, 256]])
            P.op("dve", lambda e, mg_bc=mg_bc: e.tensor_tensor(out=hm, in0=hm, in1=mg_bc, op=ALU.mult), reads=b_hm + [b_mgb], writes=b_hm)
            P.op("dve", lambda e: e.tensor_tensor(out=mo, in0=hm, in1=mo, op=ALU.mult), reads=b_hm + [b_mo], writes=[b_mo])
            odv = CONCAT[:, 1024 + hd * 256:1024 + (hd + 1) * 256].rearrange("(c p) e -> p c e", p=128)
            P.dma("pool", "sm", lambda e, odv=odv: e.dma_start(out=odv, in_=mo), reads=[b_mo], writes=[b_CC])
        P.barrier(barscr)

    if "D" in phases:
        AR.off = persist_off
        A = ffn_phase_alloc(2)
        load_gT(A["gT"][0], gv["ffn2_pre_g"], A["b_gT"][0])
        P.dma("sp", cs(), lambda e: e.dma_start(out=A["gb"][0], in_=bcast_row(gv["mix_post_g"], D)), writes=[A["b_gb"][0]])
        P.dma("sp", cs(), lambda e: e.dma_start(out=A["gb"][1], in_=bcast_row(gv["ffn2_post_g"], D)), writes=[A["b_gb"][1]])
        xr0 = A["xr"][0]
        ccv = bass.AP(xr0.tensor, xr0.offset, [list(xr0.ap[0]), [1, 4096]]).bitcast(BF16).rearrange("p (b d) -> p b d", b=4)
        b_ccb = [A["b_xr"][0], A["b_xr"][0], A["b_xr"][1], A["b_xr"][1]]
        for t in range(NT):
            csrc = CONCAT[t * 512:(t + 1) * 512, :].rearrange("(b p) d -> p b d", p=128)
            P.dma("sp", "xl", lambda e, csrc=csrc: e.dma_start(out=ccv, in_=csrc), reads=[b_CC], writes=[A["b_xr"][0], A["b_xr"][1]])
            norm_transpose(A, None, None, src=ccv, b_src=b_ccb, scale=False)
            hT = A["hT"]
            for g in range(8):
                s = A["cnt"]["wgu"] % 2
                A["cnt"]["wgu"] += 1
                w_t = A["wgs"][s]
                P.dma("sp", f"wg{s}", lambda e, g=g, w_t=w_t: e.dma_start(out=w_t, in_=swout[g]), reads=[b_wout[g]], writes=[A["b_wgs"][s]])
                for b in range(4):
                    bk = next_bank()
                    for c in range(16):
                        P.op("pe", lambda e, bk=bk, c=c, b=b, w_t=w_t: e.matmul(bank(bk)[:, 0:256], lhsT=hT[:, c, b * 128:(b + 1) * 128], rhs=w_t[:, c, :], start=(c == 0), stop=(c == 15)),
                             reads=[A["b_wgs"][s], A["b_hT"]], writes=[bps[bk]])
                    copy_op(alt_eng(), A["xy"][:, b, g * 256:(g + 1) * 256], bank(bk)[:, 0:256], [bps[bk]], [A["b_xy"][b]])
            for hf in range(2):
                post_residual(A, t, X1, b_X1, A["gb"][0], A["b_gb"][0], 1.0, X2, b_X2, False, hf)
                sumsq_blocks(A, hf)
                stats_rstd(A, D, 1.0, hf)
                norm_transpose(A, A["gT"][0], A["b_gT"][0], blocks=(2 * hf, 2 * hf + 1))
            ffn(A, sw2g, sw2u, sw2d, b_w2g, b_w2u, b_w2d)
            for hf in range(2):
                post_residual(A, t, X2, b_X2, A["gb"][1], A["b_gb"][1], 0.5, y, None, True, hf)

    if not final_ops:
        final_ops.append(P.barrier(barscr))
        for s, l in P.streams.items():
            final_ops.append(l[-1])
    P.emit_all(nc, final_ops)
    es.close()
    return nc, P


_CACHE = {}
W_NAMES = ["rel_bias", "ffn1_pre_g", "ffn1_post_g", "ffn1_w_gate", "ffn1_w_up", "ffn1_w_down", "mix_pre_g", "mix_post_g",
           "w_in", "gate_bias", "conv_w", "conv_b", "lambda_q1", "lambda_k1", "lambda_q2", "lambda_k2",
           "att_head_g", "mlstm_head_g", "w_out", "ffn2_pre_g", "ffn2_post_g", "ffn2_w_gate", "ffn2_w_up", "ffn2_w_down"]


def shared_inputs(inputs):
    sh = {}
    for n in W_NAMES:
        a = np.ascontiguousarray(np.asarray(inputs[n], dtype=np.float32))
        if n != "rel_bias":
            a = a[0]
            if a.ndim == 1:
                a = a[None, :]
        sh[n] = np.ascontiguousarray(a)
    sh.update(host_consts())
    return sh


def kernel(**inputs):
    if "nc" not in _CACHE:
        _CACHE["nc"], _ = build_program()
    nc = _CACHE["nc"]
    xp = np.asarray(inputs["x_prompt"], dtype=np.float32)
    xs = np.asarray(inputs["x_sample"], dtype=np.float32)
    seqs = [xp[0], xp[1], xs[0], xs[1], xs[2], xs[3], xp[0], xp[1]]
    sh = shared_inputs(inputs)
    in_maps = []
    for c in range(8):
        m = dict(sh)
        m["x"] = np.ascontiguousarray(seqs[c])
        in_maps.append(m)
    res = run_bass_kernel_spmd(nc, in_maps, core_ids=list(range(8)))
    outs = [np.asarray(res.results[c]["y"], dtype=np.float32) for c in range(6)]
    y_prompt = np.stack(outs[0:2], axis=0)
    y_sample = np.stack(outs[2:6], axis=0)
    return (y_prompt, y_sample)
```
